# Optimizing a Trainium2 kernel written in Bass

```python
import jax, jax.numpy as jnp
from jax import lax
import numpy as np

D_MODEL = 2048
BATCH = 4
SEQ = 4096
DEPTH = 1

CONV_WIDTH = D_MODEL
CONV_K = 3
HEAD_DIM = 128
DILATED_PATTERNS = ((128, 1), (512, 4), (2048, 16))
N_GROUPS = 3
HEADS_PER_GROUP = 8
N_HEADS = N_GROUPS * HEADS_PER_GROUP
ATTN_WIDTH = N_HEADS * HEAD_DIM
ATTN_OUT_WIDTH = HEADS_PER_GROUP * HEAD_DIM
ROT_DIM = HEAD_DIM // 4
ROPE_THETA = 500000.0
BLOCK = 128
D_FF = 5632
FFN_CONV_K = 3
EPS = 1e-6
IN_COLS = 3 * CONV_WIDTH + 3 * ATTN_WIDTH + 2 * D_MODEL

kernel_name = "hybrid_shortconv_dilated_attn_convffn"


def rms_norm(x, g):
    x32 = x.astype(jnp.float32)
    y = x32 * lax.rsqrt(jnp.mean(x32 * x32, axis=-1, keepdims=True) + EPS)
    return (y * g.astype(jnp.float32)).astype(x.dtype)


def causal_dwconv(x, w):
    k = w.shape[0]
    s = x.shape[1]
    xp = jnp.pad(x, ((0, 0), (k - 1, 0), (0, 0)))
    y = w[0] * xp[:, 0:s]
    for i in range(1, k):
        y = y + w[i] * xp[:, i:i + s]
    return y


def partial_rope(x, positions):
    half = ROT_DIM // 2
    inv_freq = jnp.float32(ROPE_THETA) ** (-jnp.arange(half, dtype=jnp.float32) * (2.0 / ROT_DIM))
    ang = positions.astype(jnp.float32)[..., None] * inv_freq
    cos = jnp.cos(ang)[:, :, None, None, :]
    sin = jnp.sin(ang)[:, :, None, None, :]
    xr = x[..., :ROT_DIM].astype(jnp.float32)
    x1, x2 = xr[..., :half], xr[..., half:]
    rot = jnp.concatenate([x1 * cos - x2 * sin, x2 * cos + x1 * sin], axis=-1)
    return jnp.concatenate([rot.astype(x.dtype), x[..., ROT_DIM:]], axis=-1)


def dilated_window_attention(q, k, v, window, dilation):
    b, s, h, hd = q.shape
    span = window // dilation
    assert span <= BLOCK
    sub_len = s // dilation
    nb = -(-sub_len // BLOCK)
    lp = nb * BLOCK

    def to_sub(t):
        t = t.reshape(b, sub_len, dilation, h, hd).transpose(0, 2, 3, 1, 4)
        t = jnp.pad(t, ((0, 0), (0, 0), (0, 0), (0, lp - sub_len), (0, 0)))
        return t.reshape(b, dilation, h, nb, BLOCK, hd)

    def with_prev(t):
        prev = jnp.pad(t[:, :, :, :-1], ((0, 0), (0, 0), (0, 0), (1, 0), (0, 0), (0, 0)))
        return jnp.concatenate([prev, t], axis=-2)

    qb = to_sub(q)
    kk = with_prev(to_sub(k))
    vv = with_prev(to_sub(v)).astype(jnp.float32)
    scores = jnp.einsum('brhnqd,brhnkd->brhnqk', qb, kk).astype(jnp.float32) * (hd ** -0.5)
    qi = jnp.arange(BLOCK)[:, None]
    ki = jnp.arange(2 * BLOCK)[None, :]
    offset = qi + BLOCK - ki
    blk = jnp.arange(nb)[:, None, None]
    valid = (offset >= 0) & (offset <= span) & ((blk > 0) | (ki >= BLOCK))
    scores = jnp.where(valid, scores, -jnp.inf)
    m = jnp.max(scores, axis=-1, keepdims=True)
    p = jnp.exp(scores - m)
    den = jnp.sum(p, axis=-1, keepdims=True)
    out = jnp.einsum('brhnqk,brhnkd->brhnqd', p, vv) / den
    lse = (m + jnp.log(den))[..., 0]

    def from_sub(t):
        extra = t.shape[5:]
        t = t.reshape(b, dilation, h, lp, *extra)[:, :, :, :sub_len]
        t = jnp.moveaxis(t, 3, 1)
        return t.reshape(b, s, h, *extra)

    return from_sub(out), from_sub(lse)


def setup_inputs(seed: int = 0) -> dict:
    key = jax.random.key(seed)
    ks = jax.random.split(key, 16)
    f32 = jnp.float32

    def normal(k, shape, fan_in):
        return jax.random.normal(k, shape, f32) * (fan_in ** -0.5)

    def gain(k, shape):
        return 1.0 + 0.01 * jax.random.normal(k, shape, f32)

    x = jax.random.normal(ks[0], (BATCH, SEQ, D_MODEL), f32)
    offset = jax.random.randint(ks[1], (BATCH, 1), 0, 1024, dtype=jnp.int32)
    positions = offset + jnp.arange(SEQ, dtype=jnp.int32)[None, :]
    return {
        "x": x,
        "positions": positions,
        "mix_norm": gain(ks[2], (DEPTH, D_MODEL)),
        "w_in": normal(ks[3], (DEPTH, D_MODEL, IN_COLS), D_MODEL),
        "conv_mix_w": normal(ks[4], (DEPTH, CONV_K, CONV_WIDTH), CONV_K),
        "w_conv_out": normal(ks[5], (DEPTH, CONV_WIDTH, D_MODEL), CONV_WIDTH),
        "q_norm": gain(ks[6], (DEPTH, HEAD_DIM)),
        "k_norm": gain(ks[7], (DEPTH, HEAD_DIM)),
        "w_attn_out": normal(ks[8], (DEPTH, ATTN_OUT_WIDTH, D_MODEL), ATTN_OUT_WIDTH),
        "w_merge_out": normal(ks[9], (DEPTH, D_MODEL, D_MODEL), D_MODEL),
        "ffn_norm": gain(ks[10], (DEPTH, D_MODEL)),
        "w_up": normal(ks[11], (DEPTH, D_MODEL, 2 * D_FF), D_MODEL),
        "ffn_conv_w": normal(ks[12], (DEPTH, FFN_CONV_K, 2 * D_FF), FFN_CONV_K),
        "w_down": normal(ks[13], (DEPTH, D_FF, D_MODEL), D_FF),
    }


def reference(x, positions, mix_norm, w_in, conv_mix_w, w_conv_out, q_norm, k_norm,
              w_attn_out, w_merge_out, ffn_norm, w_up, ffn_conv_w, w_down):
    b, s, _ = x.shape
    split_at = list(np.cumsum([CONV_WIDTH, CONV_WIDTH, CONV_WIDTH,
                               ATTN_WIDTH, ATTN_WIDTH, ATTN_WIDTH, D_MODEL]))
    for layer in range(DEPTH):
        h = rms_norm(x, mix_norm[layer])
        proj = h @ w_in[layer]
        cb, cc, cx, q, k, v, g_conv, g_attn = jnp.split(proj, split_at, axis=-1)

        conv_y = cb * causal_dwconv(cc * cx, conv_mix_w[layer])
        branch_conv = conv_y @ w_conv_out[layer]

        q = q.reshape(b, s, N_GROUPS, HEADS_PER_GROUP, HEAD_DIM)
        k = k.reshape(b, s, N_GROUPS, HEADS_PER_GROUP, HEAD_DIM)
        v = v.reshape(b, s, N_GROUPS, HEADS_PER_GROUP, HEAD_DIM)
        q = partial_rope(rms_norm(q, q_norm[layer]), positions)
        k = partial_rope(rms_norm(k, k_norm[layer]), positions)
        outs, lses = [], []
        for gi, (window, dilation) in enumerate(DILATED_PATTERNS):
            o, lse = dilated_window_attention(q[:, :, gi], k[:, :, gi], v[:, :, gi], window, dilation)
            outs.append(o)
            lses.append(lse)
        outs = jnp.stack(outs, axis=2)
        wts = jax.nn.softmax(jnp.stack(lses, axis=2), axis=2)
        attn = jnp.sum(wts[..., None] * outs, axis=2).astype(x.dtype)
        branch_attn = attn.reshape(b, s, ATTN_OUT_WIDTH) @ w_attn_out[layer]

        merged = jax.nn.sigmoid(g_conv) * branch_conv + jax.nn.sigmoid(g_attn) * branch_attn
        x = x + merged @ w_merge_out[layer]

        h = rms_norm(x, ffn_norm[layer])
        up = causal_dwconv(h @ w_up[layer], ffn_conv_w[layer])
        gate, val = jnp.split(up, 2, axis=-1)
        x = x + (jax.nn.silu(gate) * val) @ w_down[layer]
    return x
```

```python
import math
from contextlib import ExitStack

import numpy as np
import concourse.bass as bass
import concourse.mybir as mybir
from concourse.bass_utils import run_bass_kernel_spmd

ACT = mybir.ActivationFunctionType
ALU = mybir.AluOpType
F32 = mybir.dt.float32
BF16 = mybir.dt.bfloat16
I32 = mybir.dt.int32


class Sched:
    BLOCK_ATTR = {"pe": "tensor", "act": "scalar", "dve": "vector", "pool": "gpsimd", "sp": "sync"}

    def __init__(self, nc, n_dma_sems=8):
        self.nc = nc
        self.ops = []
        self.n_dma_sems = n_dma_sems
        self.last_touch = {}
        self.inorder = ("pe",)

    def op(self, eng, fn, reads=(), writes=(), dma=False, pos=None):
        o = {"eng": eng, "fn": fn, "reads": tuple(reads), "writes": tuple(writes), "dma": dma,
             "pos": (len(self.ops) if pos is None else pos - 0.5), "seq": len(self.ops)}
        if pos is None:
            for k in o["reads"] + o["writes"]:
                self.last_touch[k] = len(self.ops)
        self.ops.append(o)

    def build(self):
        nc = self.nc
        ops = sorted(self.ops, key=lambda o: (o["pos"], o["seq"]))
        last_writer = {}
        readers = {}
        for i, o in enumerate(ops):
            deps = set()
            for k in o["reads"]:
                if k in last_writer:
                    deps.add(last_writer[k])
                if isinstance(k, tuple) and k[0] == "ps":
                    for r in readers.get(k, ()):
                        if ops[r]["eng"] != o["eng"]:
                            deps.add(r)
            for k in o["writes"]:
                if k in last_writer:
                    deps.add(last_writer[k])
                for r in readers.get(k, ()):
                    deps.add(r)
            deps.discard(i)
            if o["eng"] in self.inorder:
                deps = {d for d in deps if ops[d]["eng"] != o["eng"] or ops[d]["dma"] != o["dma"]}
            o["deps"] = deps
            for k in o["reads"]:
                readers.setdefault(k, []).append(i)
            for k in o["writes"]:
                last_writer[k] = i
                readers[k] = []
        last_of = {}
        for i, o in enumerate(ops[:-1]):
            last_of[o["eng"]] = i
        for e_, i in last_of.items():
            if i != len(ops) - 1:
                ops[-1]["deps"].add(i)
        for o in ops:
            o["signal"] = o["dma"]
        for o in ops:
            for d in o["deps"]:
                ops[d]["signal"] = True
        with ExitStack() as es:
            esem = {e: es.enter_context(nc.semaphore("s_" + e)) for e in ("pe", "act", "dve", "pool")}
            dsem = {q: [es.enter_context(nc.semaphore("d_%s%d" % (q, j))) for j in range(self.n_dma_sems)]
                    for q in ("sp", "pool")}
            cnt = {e: 0 for e in esem}
            dcnt = {q: 0 for q in dsem}
            for o in ops:
                if o["dma"]:
                    q = o["eng"]
                    j = dcnt[q] % self.n_dma_sems
                    o["sem"] = dsem[q][j]
                    o["val"] = 16 * (dcnt[q] // self.n_dma_sems + 1)
                    o["prev"] = o["val"] - 16
                    dcnt[q] += 1
                elif o["signal"]:
                    cnt[o["eng"]] += 1
                    o["sem"] = esem[o["eng"]]
                    o["val"] = cnt[o["eng"]]
            by_eng = {}
            for i, o in enumerate(ops):
                by_eng.setdefault(o["eng"], []).append(i)
            self.stats = {e: len(v) for e, v in by_eng.items()}
            nwaits = [0]

            def make_body(eng):
                def body(e):
                    waited = {}
                    for i in by_eng.get(eng, ()):
                        o = ops[i]
                        need = {}
                        for d in o["deps"]:
                            p = ops[d]
                            key = id(p["sem"])
                            if need.get(key, (None, 0))[1] < p["val"]:
                                need[key] = (p["sem"], p["val"])
                        if o["dma"] and o["prev"] > 0:
                            key = id(o["sem"])
                            if need.get(key, (None, 0))[1] < o["prev"]:
                                need[key] = (o["sem"], o["prev"])
                        for key, (s, v) in need.items():
                            if waited.get(key, 0) >= v:
                                continue
                            e.wait_ge(s, v)
                            waited[key] = v
                            nwaits[0] += 1
                        ins = o["fn"](e)
                        if o["dma"]:
                            ins.then_inc(o["sem"], 16)
                        elif o["signal"]:
                            ins.then_inc(o["sem"], 1)
                return body

            with nc.Block() as block:
                for eng, attr in self.BLOCK_ATTR.items():
                    getattr(block, attr)(make_body(eng))
            self.stats["waits"] = nwaits[0]
        return nc


class Cfg:
    def __init__(self, D=2048, HPG=8, DFF=5632, FB=4):
        self.D, self.HPG, self.DFF, self.FB = D, HPG, DFF, FB
        self.NC = D // 128
        self.AW = 3 * HPG * 128
        self.AO = HPG * 128
        self.FC = DFF // 128
        self.FCB = self.FC // FB
        assert self.FCB * FB == self.FC
        self.IN_COLS = 3 * D + 3 * self.AW + 2 * D
        self.NT = 768
        self.halo_sts = [(-2048, 768), (-1280, 768), (-512, 384)]
        self.main_sts = [(-128, 768), (640, 768), (1408, 640)]
        self.NSP = 5 * self.NC + 6 * self.FC + 3


DILS = (1, 4, 16)
KV_LO = (-256, -1024, -2048)
NBMIN = (-2, -2, -1)


def split_tiles(lo, hi):
    n = hi - lo
    k = -(-n // 384)
    nb = n // 128
    out = []
    o = lo
    for i in range(k):
        s = (nb // k + (1 if i < nb % k else 0)) * 128
        out.append((o, s))
        o += s
    return out


def build_program(cfg):
    D, NC, HPG, DFF, FC, FCB, FB, NT = cfg.D, cfg.NC, cfg.HPG, cfg.DFF, cfg.FC, cfg.FCB, cfg.FB, cfg.NT
    AW, AO = cfg.AW, cfg.AO
    NB = NT // 128
    nc = bass.Bass("TRN2", target_bir_lowering=False)

    def din(name, shape, dt=F32):
        return nc.dram_tensor(name, shape, dt, kind="ExternalInput").ap()

    xe = din("xe", [4096, D])
    pos = din("pos", [1, 4096], I32)
    w_in = din("w_in", [D, cfg.IN_COLS])
    w_co = din("w_co", [D, D])
    w_ao = din("w_ao", [AO, D])
    w_mo = din("w_mo", [D, D])
    w_up = din("w_up", [D, 2 * DFF])
    w_dn = din("w_dn", [DFF, D])
    smallp = din("smallp", [128, cfg.NSP])
    cpack = din("cpack", [128, 6 * 128])
    out = nc.dram_tensor("out", [2048, D], F32, kind="ExternalOutput").ap()
    kcache = nc.dram_tensor("kcache", [3 * HPG, 128, 4096], BF16).ap()
    vcache = nc.dram_tensor("vcache", [3, 32, 128, HPG * 128], BF16).ap()

    sb = nc.alloc_sbuf_tensor
    big = sb("big", [128, NB * D], F32)
    x1v = big[:].rearrange("p (b f) -> p b f", b=NB)
    bigbf = big[:].bitcast(BF16).rearrange("p (a c t) -> p a c t", a=2, c=NC)
    hT = bigbf[:, 0]
    convy = bigbf[:, 1]
    R3N = max(NC * NT, 8192)
    r3 = sb("r3", [128, R3N], BF16)
    r3v = r3[:, 0:NC * NT].rearrange("p (c t) -> p c t", c=NC)
    kTts = [r3[:, 0:4096], r3[:, 8192:12288]] if R3N >= 12288 else [r3[:, 0:4096], r3[:, 0:4096]]
    KDB = 2 if R3N >= 12288 else 1
    vtt = r3[:, 4096:8192].rearrange("p (b d) -> p b d", b=32)
    R4N = max(HPG * NT + 6 * NT, FCB * NT)
    r4 = sb("r4", [128, R4N], BF16)
    attnT = r4[:, 0:HPG * NT].rearrange("p (h t) -> p h t", h=HPG)
    qTs = [r4[:, HPG * NT + i * 3 * NT:HPG * NT + (i + 1) * 3 * NT].rearrange("p (g t) -> p g t", g=3) for i in range(2)]
    actT = r4[:, 0:FCB * NT].rearrange("p (k t) -> p k t", k=FCB)
    KCM = max(NC, FCB, HPG)
    NSLOT = 8
    wsl = sb("wsl", [128, KCM, NSLOT * 128], BF16)
    cp = sb("cp", [128, 6 * 128], BF16)
    ident, RTm, mprev, mcur, ones, hmat = [cp[:, i * 128:(i + 1) * 128] for i in range(6)]
    sp_sb = sb("sp_sb", [128, cfg.NSP], F32)
    mixg = sp_sb[:, 0:NC]
    ffng = sp_sb[:, NC:2 * NC]
    convw = sp_sb[:, 2 * NC:5 * NC]
    fconvw = sp_sb[:, 5 * NC:5 * NC + 6 * FC]
    o0 = 5 * NC + 6 * FC
    gq = sp_sb[:, o0:o0 + 1]
    gk = sp_sb[:, o0 + 1:o0 + 2]
    invf = sp_sb[:, o0 + 2:o0 + 3]
    eps = sb("eps", [128, 1], F32)
    Ctab = sb("Ctab", [128, NT], F32)
    Stab = sb("Stab", [128, NT], F32)
    xbs = [sb("xb%d" % i, [128, D], F32) for i in range(2)]
    hbs = [sb("hb%d" % i, [128, D], BF16) for i in range(2)]
    ss = sb("ss", [128, 2], F32)
    ln1 = sb("ln1", [128, 2], F32)
    rs1 = sb("rs1", [128, 2], F32)
    TW = 384
    NSET = 3
    ybuf = [sb("ybuf%d" % i, [128, TW], BF16) for i in range(NSET)]
    sqbuf = [sb("sqbuf%d" % i, [128, TW], BF16) for i in range(NSET)]
    lnv = [sb("lnv%d" % i, [128, TW], F32) for i in range(NSET)]
    t1b = [sb("t1b%d" % i, [128, TW], F32) for i in range(NSET)]
    t2b = [sb("t2b%d" % i, [128, TW], F32) for i in range(NSET)]
    ubufA = sb("ubufA", [128, NT + 2], F32)
    ubufB = sb("ubufB", [128, NT + 2], F32)
    yA = sb("yA", [128, NT + 2], F32)
    yB = sb("yB", [128, NT + 2], F32)
    cxs = sb("cxs", [128, TW], F32)
    ptmp = sb("ptmp", [128, NT + 2], F32)
    pt2 = sb("pt2", [128, NT + 2], F32)
    ksts = [sb("kst%d" % i, [128, NT], BF16) for i in range(2)]
    vTs = sb("vTs", [128, NT], BF16)
    vst = [sb("vst%d" % i, [128, 8, 128], BF16) for i in range(2)]
    pTp = sb("pTp", [128, 512], BF16)
    pTc = sb("pTc", [128, 512], BF16)
    pTps = [pTp[:, :], vst[0][:, 0:4, :].rearrange("p a d -> p (a d)")]
    pTcs = [pTc[:, :], vst[1][:, 0:4, :].rearrange("p a d -> p (a d)")]
    ucarry = sb("ucarry", [128, NC, 3], F32)
    fcarry = sb("fcarry", [128, 2 * FC, 3], F32)
    ps = [nc.alloc_psum_tensor("ps%d" % i, [128, 512], F32) for i in range(8)]

    S = Sched(nc)
    S.inorder = getattr(cfg, "inorder", ("pe", "dve"))
    DBG = getattr(cfg, "dbg", ())
    state = {"p": 0, "wi": 0, "anchor": [], "tog": 0, "vs": 0, "xi": 0, "qi": 0, "ki": 0, "ksi": 0, "pti": 0, "pending": []}

    def bk(name):
        return [(name, 0), (name, 1)]

    def pnext():
        i = state["p"]
        state["p"] = (i + 1) % 8
        return i

    def psbf(i):
        return ps[i][:].bitcast(BF16).rearrange("p (u d) -> p u d", u=8)

    PD = NSLOT - 2

    def wget(dram_ap, kc, ncols):
        ns = ncols // 128
        i = state["wi"]
        if (i % NSLOT) + ns > NSLOT:
            i += NSLOT - (i % NSLOT)
        state["wi"] = i + ns
        s = i % NSLOT
        keys = [("w", s + j) for j in range(ns)]
        anc = state["anchor"]
        posn = anc[len(anc) - PD] if len(anc) >= PD else 0
        for kk_ in keys:
            posn = max(posn, S.last_touch.get(kk_, -1) + 1)
        anc.append(len(S.ops))
        dst = wsl[:, 0:kc, s * 128:(s + ns) * 128]
        src = dram_ap.rearrange("(c p) n -> p c n", p=128)
        S.op("pool", lambda e, dst=dst, src=src: e.dma_start(out=dst, in_=src), writes=keys, dma=True, pos=posn)

        def wf(k, j=None):
            if j is None:
                return wsl[:, k, s * 128:(s + ns) * 128]
            return wsl[:, k, (s + j) * 128:(s + j + 1) * 128]

        return wf, keys

    def mmgroup(out_ap, out_key, terms):
        n = len(terms)
        for i, (l, r, keys) in enumerate(terms):
            S.op("pe", lambda e, l=l, r=r, i=i: e.matmul(out_ap, l, r, start=(i == 0), stop=(i == n - 1)),
                 reads=keys, writes=[out_key])

    def proj(wf, wkeys, actT_, akeys, kcn, off, nn, j=None):
        pi = pnext()
        mmgroup(ps[pi][:, 0:nn], ("ps", pi),
                [(wf(k, j), actT_[:, k, off:off + nn], list(wkeys) + list(akeys)) for k in range(kcn)])
        flush()
        return pi

    def blk_keys(name, off, nn):
        return [(name, b) for b in range(off // 128, (off + nn) // 128)]

    S.op("pool", lambda e: e.dma_start(out=cp[:], in_=cpack), writes=["cp"], dma=True)
    S.op("sp", lambda e: e.dma_start(out=sp_sb[:], in_=smallp), writes=["spb"], dma=True)
    S.op("dve", lambda e: e.memset(eps[:], 1e-6), writes=["eps"])
    S.op("dve", lambda e: e.memset(ucarry[:], 0.0), writes=["ucarry"])
    S.op("dve", lambda e: e.memset(fcarry[:], 0.0), writes=["fcarry"])

    def build_hT(src_fn, src_keys_fn, gcols, dstT, dkey, nblk, load_fn=None):
        for b in range(nblk):
            xi = state["xi"]
            state["xi"] = 1 - xi
            hb = hbs[xi]
            hk = ("hb", xi)
            if load_fn is not None:
                load_fn(b, xi)
            src = src_fn(b, xi)
            sk = src_keys_fn(b, xi)
            ssx, ln1x, rs1x = ss[:, xi:xi + 1], ln1[:, xi:xi + 1], rs1[:, xi:xi + 1]
            S.op("dve", lambda e, ssx=ssx: e.memset(ssx, 0.0), writes=[("ss", xi)])
            S.op("act", lambda e, src=src, hb=hb, ssx=ssx: e.activation(out=hb[:], in_=src, func=ACT.Square, accum_out=ssx),
                 reads=sk + [("ss", xi)], writes=[hk, ("ss", xi)])
            S.op("act", lambda e, ssx=ssx, ln1x=ln1x: e.activation(out=ln1x, in_=ssx, func=ACT.Ln, scale=1.0 / D, bias=eps[:, 0:1]),
                 reads=[("ss", xi), "eps"], writes=[("ln1", xi)])
            S.op("act", lambda e, ln1x=ln1x, rs1x=rs1x: e.activation(out=rs1x, in_=ln1x, func=ACT.Exp, scale=-0.5),
                 reads=[("ln1", xi)], writes=[("rs1", xi)])
            S.op("dve", lambda e, src=src, hb=hb, rs1x=rs1x: e.tensor_scalar(hb[:], src, rs1x, None, ALU.mult),
                 reads=sk + [("rs1", xi)], writes=[hk])
            for c0 in range(0, NC, 8):
                pi = pnext()
                pb = psbf(pi)
                cn = min(8, NC - c0)
                for c in range(c0, c0 + cn):
                    S.op("pe", lambda e, c=c, pb=pb, c0=c0, hb=hb: e.transpose(pb[:, c - c0, :], hb[:, c * 128:(c + 1) * 128], ident),
                         reads=[hk, "cp"], writes=[("ps", pi)])
                for c in range(c0, c0 + cn):
                    dst = dstT[:, c, b * 128:(b + 1) * 128]
                    if c % 2 == 0:
                        S.op("act", lambda e, dst=dst, pb=pb, c=c, c0=c0: e.activation(
                            out=dst, in_=pb[:, c - c0, :], func=ACT.Copy, scale=gcols[:, c:c + 1]),
                            reads=[("ps", pi), "spb"], writes=[(dkey, b)])
                    else:
                        S.op("dve", lambda e, dst=dst, pb=pb, c=c, c0=c0: e.tensor_scalar(
                            dst, pb[:, c - c0, :], gcols[:, c:c + 1], None, ALU.mult),
                            reads=[("ps", pi), "spb"], writes=[(dkey, b)])
            yield b

    def run(gen):
        for _ in gen:
            pass

    def build_tables(l0, n):
        posi = ubufB[:, 0:n].bitcast(I32)
        ki = ubufA[:, 0:n].bitcast(I32)
        ang = yA[:, 0:n]
        kf = yB[:, 0:n]
        TWO_PI = 2.0 * math.pi
        C1 = 6.28125
        C2 = TWO_PI - C1
        S.op("sp", lambda e: e.dma_start(out=posi, in_=pos[0:1, l0 + 2048:l0 + 2048 + n].broadcast_to([128, n])),
             writes=[*bk("ubufB")], dma=True)
        for which, tab in ((0, Stab), (1, Ctab)):
            S.op("dve", lambda e: e.tensor_copy(ang, posi), reads=[*bk("ubufB")], writes=[*bk("yA")])
            if which == 0:
                S.op("dve", lambda e: e.tensor_scalar(ang, ang, invf, None, ALU.mult), reads=[*bk("yA"), "spb"], writes=[*bk("yA")])
            else:
                S.op("dve", lambda e: e.tensor_scalar(ang, ang, invf, 0.5 * math.pi, ALU.mult, ALU.add),
                     reads=[*bk("yA"), "spb"], writes=[*bk("yA")])
            S.op("dve", lambda e: e.tensor_scalar(kf, ang, 1.0 / TWO_PI, None, ALU.mult), reads=[*bk("yA")], writes=[*bk("yB")])
            S.op("dve", lambda e: e.tensor_copy(ki, kf), reads=[*bk("yB")], writes=[*bk("ubufA")])
            S.op("dve", lambda e: e.tensor_copy(kf, ki), reads=[*bk("ubufA")], writes=[*bk("yB")])
            S.op("dve", lambda e: e.scalar_tensor_tensor(ang, kf, -C1, ang, ALU.mult, ALU.add), reads=[*bk("yB"), *bk("yA")], writes=[*bk("yA")])
            S.op("dve", lambda e: e.scalar_tensor_tensor(ang, kf, -C2, ang, ALU.mult, ALU.add), reads=[*bk("yB"), *bk("yA")], writes=[*bk("yA")])
            S.op("dve", lambda e: e.tensor_scalar(kf, ang, math.pi, -TWO_PI, ALU.is_gt, ALU.mult), reads=[*bk("yA")], writes=[*bk("yB")])
            S.op("dve", lambda e: e.tensor_tensor(ang, ang, kf, ALU.add), reads=[*bk("yA"), *bk("yB")], writes=[*bk("yA")])
            S.op("dve", lambda e: e.tensor_scalar(kf, ang, -math.pi, TWO_PI, ALU.is_lt, ALU.mult), reads=[*bk("yA")], writes=[*bk("yB")])
            S.op("dve", lambda e: e.tensor_tensor(ang, ang, kf, ALU.add), reads=[*bk("yA"), *bk("yB")], writes=[*bk("yA")])
            S.op("dve", lambda e: e.tensor_scalar(ang, ang, 3.1415925, -3.1415925, ALU.min, ALU.max), reads=[*bk("yA")], writes=[*bk("yA")])
            S.op("act", lambda e, tab=tab: e.activation(out=tab[:, 0:n], in_=ang, func=ACT.Sin),
                 reads=[*bk("yA")], writes=["tab%d" % which])

    def defer(fn):
        state["pending"].append(fn)

    def flush():
        while state["pending"]:
            state["pending"].pop(0)()

    def normrope(pi, nn, off, gcol, dest, dkeys, dil):
        t = state["tog"]
        state["tog"] = (t + 1) % NSET
        xq = ps[pi][:, 0:nn]
        yb_, sq_, ln_, t1_, t2_ = ybuf[t][:, 0:nn], sqbuf[t][:, 0:nn], lnv[t][:, 0:nn], t1b[t][:, 0:nn], t2b[t][:, 0:nn]
        rs_ = ln_
        S.op("act", lambda e: e.activation(out=yb_, in_=xq, func=ACT.Copy, scale=gcol), reads=[("ps", pi), "spb"], writes=[("yb", t)])
        S.op("act", lambda e: e.activation(out=sq_, in_=xq, func=ACT.Square), reads=[("ps", pi)], writes=[("sq", t)])
        S.op("dve", lambda e: e.scalar_tensor_tensor(t1_, xq, gcol, Ctab[:, off:off + nn], ALU.mult, ALU.mult),
             reads=[("ps", pi), "spb", "tab1"], writes=[("t1", t)])

        def part_b():
            pj = pnext()
            S.op("pe", lambda e: e.matmul(ps[pj][:, 0:nn], ones, sq_, start=True, stop=True), reads=[("sq", t), "cp"], writes=[("ps", pj)])
            pk = pnext()
            S.op("pe", lambda e: e.matmul(ps[pk][:, 0:nn], RTm, yb_, start=True, stop=True), reads=[("yb", t), "cp"], writes=[("ps", pk)])
            S.op("act", lambda e: e.activation(out=ln_, in_=ps[pj][:, 0:nn], func=ACT.Ln, scale=1.0 / 128, bias=eps[:, 0:1]),
                 reads=[("ps", pj), "eps"], writes=[("ln", t)])
            S.op("act", lambda e: e.activation(out=rs_, in_=ln_, func=ACT.Exp, scale=-0.5), reads=[("ln", t)], writes=[("ln", t)])
            S.op("dve", lambda e: e.tensor_tensor(t2_, ps[pk][:, 0:nn], Stab[:, off:off + nn], ALU.mult),
                 reads=[("ps", pk), "tab0"], writes=[("t2", t)])
            S.op("dve", lambda e: e.tensor_tensor(t1_, t1_, t2_, ALU.add), reads=[("t1", t), ("t2", t)], writes=[("t1", t)])
            if dil == 1:
                a_, b_ = t1_, rs_
            else:
                a_ = t1_.rearrange("p (i r) -> p r i", r=dil)
                b_ = rs_.rearrange("p (i r) -> p r i", r=dil)
            S.op("dve", lambda e: e.tensor_tensor(dest, a_, b_, ALU.mult), reads=[("t1", t), ("ln", t)], writes=dkeys)

        defer(part_b)

    def kv_phase(l0, n, hTb, hkey, bg=None):
        for g in range(3):
            dil = DILS[g]
            lo = max(l0, KV_LO[g])
            hi = l0 + n
            if lo >= hi:
                continue
            tiles = split_tiles(lo, hi)
            ng = hi - lo
            i0, i1 = lo // dil, hi // dil
            for h in range(HPG):
                gh = g * HPG + h
                ksi = state["ksi"]
                state["ksi"] = 1 - ksi
                kst = ksts[ksi]
                kkey_ = ("kst", ksi)
                if h % 2 == 0:
                    np_ = min(2, HPG - h)
                    wfK, wkK = wget(w_in[:, 3 * D + AW + gh * 128:3 * D + AW + (gh + np_) * 128], NC, np_ * 128)
                    wfV, wkV = wget(w_in[:, 3 * D + 2 * AW + gh * 128:3 * D + 2 * AW + (gh + np_) * 128], NC, np_ * 128)
                jh = h % 2
                wf, wk = wfK, wkK
                kview = kst[:, 0:ng] if dil == 1 else kst[:, 0:ng].rearrange("p (r i) -> p r i", r=dil)
                for (tl, nn) in tiles:
                    off = tl - l0
                    pi = proj(wf, wk, hTb, blk_keys(hkey, off, nn), NC, off, nn, j=jh)
                    o2 = (tl - lo) // dil
                    dest = kview[:, o2:o2 + nn] if dil == 1 else kview[:, :, o2:o2 + nn // dil]
                    normrope(pi, nn, off, gk, dest, [kkey_], dil)
                kc_v = kcache[gh].rearrange("d (r i) -> d r i", r=dil)[:, :, i0 + 2048 // dil:i1 + 2048 // dil]
                ksrc = kst[:, 0:ng].rearrange("p (r i) -> p r i", r=dil)
                def k_dma(kc_v=kc_v, ksrc=ksrc, kkey_=kkey_, gh=gh):
                    S.op("sp", lambda e: e.dma_start(out=kc_v, in_=ksrc), reads=[kkey_], writes=[("kc", gh)], dma=True)
                defer(k_dma)
                if "nov" in DBG:
                    continue
                if bg is not None:
                    next(bg, None)
                wf, wk = wfV, wkV
                for (tl, nn) in tiles:
                    off = tl - l0
                    pi = proj(wf, wk, hTb, blk_keys(hkey, off, nn), NC, off, nn, j=jh)
                    S.op("act", lambda e, pi=pi, off=off, nn=nn: e.activation(out=vTs[:, off:off + nn], in_=ps[pi][:, 0:nn], func=ACT.Copy),
                         reads=[("ps", pi)], writes=["vTs"])
                def v_part(g=g, h=h, gh=gh, dil=dil, i0=i0, i1=i1):
                    NBR = 32 // dil
                    nbc = -16 // dil
                    for nb in range(i0 // 128, (i1 - 1) // 128 + 1):
                        ka = max(i0, 128 * nb) - 128 * nb
                        kb = min(i1, 128 * nb + 128) - 128 * nb
                        m = kb - ka
                        units = [(r, nb) for r in range(dil)]
                        for u0 in range(0, len(units), 8):
                            us = units[u0:u0 + 8]
                            U = len(us)
                            pi = pnext()
                            pb = psbf(pi)
                            for u, (r, nb_) in enumerate(us):
                                c0 = dil * (128 * nb_ + ka) + r - l0
                                src = vTs[:, c0:c0 + dil * (m - 1) + 1:dil]
                                S.op("pe", lambda e, pb=pb, u=u, src=src, m=m: e.transpose(pb[0:m, u, :], src, ident),
                                     reads=["vTs", "cp"], writes=[("ps", pi)])
                            vs = state["vs"]
                            state["vs"] = 1 - vs
                            S.op("dve", lambda e, pb=pb, vs=vs, m=m, U=U: e.tensor_copy(vst[vs][0:m, 0:U, :], pb[0:m, 0:U, :]),
                                 reads=[("ps", pi)], writes=[("vst", vs)])
                            b0 = us[0][0] * NBR + (nb - nbc)
                            dst = vcache[g, b0:b0 + (U - 1) * NBR + 1:NBR, ka:kb, h * 128:(h + 1) * 128].rearrange("u k d -> k u d")
                            if "novdma" not in DBG:
                                S.op("sp", lambda e, dst=dst, vs=vs, m=m, U=U: e.dma_start(out=dst, in_=vst[vs][0:m, 0:U, :]),
                                     reads=[("vst", vs)], writes=[("vc", gh)], dma=True)
                defer(v_part)
        flush()


    def conv3(src_fn, skeys, A2, A1, A0, kA2, kA1, kA0, w0, w1, w2, off, nn, ti):
        S.op("act", lambda e: e.activation(out=A2[:, off:off + nn], in_=src_fn(), func=ACT.Copy, scale=w2), reads=skeys + ["spb"], writes=[(kA2, ti)])
        S.op("act", lambda e: e.activation(out=A1[:, off + 1:off + 1 + nn], in_=src_fn(), func=ACT.Copy, scale=w1), reads=skeys + ["spb"], writes=[(kA1, ti)])
        S.op("dve", lambda e: e.tensor_scalar(A0[:, off + 2:off + 2 + nn], src_fn(), w0, None, ALU.mult), reads=skeys + ["spb"], writes=[(kA0, ti)])

    def conv_phase(l0, n, tiles):
        for c in range(NC):
            wfb, wkb = wget(w_in[:, c * 128:(c + 1) * 128], NC, 128)
            wfc, wkc = wget(w_in[:, D + c * 128:D + (c + 1) * 128], NC, 128)
            wfx, wkx = wget(w_in[:, 2 * D + c * 128:2 * D + (c + 1) * 128], NC, 128)
            w0, w1, w2 = [convw[:, c * 3 + k:c * 3 + k + 1] for k in range(3)]
            S.op("pool", lambda e, c=c: e.tensor_copy(ubufA[:, 0:1], ucarry[:, c, 0:1]), reads=["ucarry"], writes=[*bk("ubufA")])
            S.op("pool", lambda e, c=c: e.tensor_copy(ptmp[:, 0:2], ucarry[:, c, 1:3]), reads=["ucarry"], writes=[*bk("ptmp")])
            for ti, (off, nn) in enumerate(tiles):
                t = state["tog"]
                state["tog"] = (t + 1) % NSET
                ak = blk_keys("hT", off, nn)
                pcb = proj(wfb, wkb, hT, ak, NC, off, nn)
                pcc = proj(wfc, wkc, hT, ak, NC, off, nn)
                pcx = proj(wfx, wkx, hT, ak, NC, off, nn)
                S.op("act", lambda e, pcx=pcx, nn=nn: e.activation(out=cxs[:, 0:nn], in_=ps[pcx][:, 0:nn], func=ACT.Copy),
                     reads=[("ps", pcx)], writes=["cxs"])
                ut = t1b[t]
                S.op("dve", lambda e, pcc=pcc, nn=nn, ut=ut: e.tensor_tensor(ut[:, 0:nn], ps[pcc][:, 0:nn], cxs[:, 0:nn], ALU.mult),
                     reads=[("ps", pcc), "cxs"], writes=[("t1", t)])
                conv3(lambda ut=ut, nn=nn: ut[:, 0:nn], [("t1", t)], yA, ubufA, ptmp, "yA", "ubufA", "ptmp", w0, w1, w2, off, nn, ti)
                S.op("act", lambda e, pcb=pcb, off=off, nn=nn: e.activation(out=yB[:, off:off + nn], in_=ps[pcb][:, 0:nn], func=ACT.Copy),
                     reads=[("ps", pcb)], writes=[("yB", ti)])
                yield ("tile", c, ti)
            S.op("dve", lambda e: e.tensor_tensor(yA[:, 0:n], yA[:, 0:n], ubufA[:, 0:n], ALU.add), reads=[*bk("yA"), *bk("ubufA")], writes=[*bk("yA")])
            S.op("dve", lambda e: e.tensor_tensor(yA[:, 0:n], yA[:, 0:n], ptmp[:, 0:n], ALU.add), reads=[*bk("yA"), *bk("ptmp")], writes=[*bk("yA")])
            S.op("pool", lambda e, c=c: e.tensor_tensor(convy[:, c, 0:n], yA[:, 0:n], yB[:, 0:n], ALU.mult),
                 reads=[*bk("yA"), *bk("yB")], writes=[("cy", c)])
            S.op("pool", lambda e, c=c: e.tensor_copy(ucarry[:, c, 0:1], ubufA[:, n:n + 1]), reads=[*bk("ubufA")], writes=["ucarry"])
            S.op("pool", lambda e, c=c: e.tensor_copy(ucarry[:, c, 1:3], ptmp[:, n:n + 2]), reads=[*bk("ptmp")], writes=["ucarry"])
            yield c

    SC = 1.0 / math.sqrt(128.0)

    def q_gen(l0, n, tiles, hs, qi):
        qT = qTs[qi]
        for g in range(3):
            gh = g * HPG + hs
            wf, wk = wget(w_in[:, 3 * D + gh * 128:3 * D + (gh + 1) * 128], NC, 128)
            for (off, nn) in tiles:
                pi = proj(wf, wk, hT, blk_keys("hT", off, nn), NC, off, nn)
                normrope(pi, nn, off, gq, qT[:, g, off:off + nn], [("qT", qi, g)], 1)
                yield (hs, g, off)

    class _Fill:
        def __init__(self, gens):
            self.gens = [g_ for g_ in gens if g_ is not None]

        def step(self):
            for g_ in self.gens:
                if next(g_, None) is not None:
                    return True
            return False

    def attn_phase(l0, n, tiles, hs, qi, fill):
        bg = fill
        qT = qTs[qi]
        fill.step()
        flush()
        numacc, denacc = pt2[:, 0:n], ubufB[:, 0:n]
        S.op("pool", lambda e: e.memset(numacc, 0.0), writes=[*bk("pt2")])
        S.op("pool", lambda e: e.memset(denacc, 0.0), writes=[*bk("ubufB")])
        for g in range(3):
            dil = DILS[g]
            gh = g * HPG + hs
            if g > 0:
                fill.step()
            flush()
            kslot = state["ki"] % KDB
            state["ki"] += 1
            kTt = kTts[kslot]
            kkey = ("kTt", kslot)
            S.op("sp", lambda e, gh=gh, kTt=kTt: e.dma_start(out=kTt, in_=kcache[gh]), reads=[("kc", gh)], writes=[kkey], dma=True)
            S.op("sp", lambda e, g=g: e.dma_start(out=vtt, in_=vcache[g, :, :, hs * 128:(hs + 1) * 128].rearrange("b k d -> k b d")),
                 reads=[("vc", g * HPG + hs)], writes=["vtt"], dma=True)
            i0, i1 = l0 // dil, (l0 + n) // dil
            NBR = 32 // dil
            nbc = -16 // dil
            RL = 4096 // dil

            def kidx(r, i):
                return r * RL + i + 2048 // dil

            def bidx(r, nb):
                return r * NBR + (nb - nbc)

            chunks = []
            for nb in range(i0 // 128, (i1 - 1) // 128 + 1):
                qa = max(i0, 128 * nb) - 128 * nb
                qb = min(i1, 128 * nb + 128) - 128 * nb
                chunks.append((nb, qa, qb))
            batches = []
            if dil == 1:
                umax = 4
                for u0 in range(0, len(chunks), umax):
                    cs = chunks[u0:u0 + umax]
                    batches.append(([(0, c[0]) for c in cs], cs[0][1], cs[0][2]))
            else:
                for (nb, qa, qb) in chunks:
                    m = qb - qa
                    umax = max(1, 512 // m)
                    for r0 in range(0, dil, umax):
                        batches.append(([(r, nb) for r in range(r0, min(dil, r0 + umax))], qa, qb))
            for (units, qa, qb) in batches:
                m = qb - qa
                U = len(units)
                nb0 = units[0][1]
                has_prev = (nb0 - 1) >= NBMIN[g]
                psc = pnext()
                psp = pnext() if has_prev else None
                pti = state["pti"]
                state["pti"] = 1 - pti
                pTp_ = pTps[pti]
                pTc_ = pTcs[pti]
                kp_ = ("pTp", 0) if pti == 0 else ("vst", 0)
                kc2_ = ("pTc", 0) if pti == 0 else ("vst", 1)
                for u, (r, nb) in enumerate(units):
                    c0 = dil * (128 * nb + qa) + r - l0
                    qcols = qT[:, g, c0:c0 + dil * (m - 1) + 1:dil]
                    if has_prev:
                        kp = kTt[:, kidx(r, 128 * (nb - 1)):kidx(r, 128 * (nb - 1)) + 128]
                        S.op("pe", lambda e, psp=psp, u=u, kp=kp, qcols=qcols, m=m: e.matmul(ps[psp][:, u * m:(u + 1) * m], kp, qcols, start=True, stop=True),
                             reads=[kkey, ("qT", qi, g)], writes=[("ps", psp)])
                    kc_ = kTt[:, kidx(r, 128 * nb):kidx(r, 128 * nb) + qb]
                    S.op("pe", lambda e, psc=psc, u=u, kc_=kc_, qcols=qcols, m=m, qb=qb: e.matmul(ps[psc][0:qb, u * m:(u + 1) * m], kc_, qcols, start=True, stop=True),
                         reads=[kkey, ("qT", qi, g)], writes=[("ps", psc)])
                if has_prev:
                    S.op("act", lambda e, psp=psp, U=U, m=m, pTp_=pTp_: e.activation(out=pTp_[:, 0:U * m], in_=ps[psp][:, 0:U * m], func=ACT.Exp, scale=SC),
                         reads=[("ps", psp)], writes=[kp_])
                    mk = mprev[:, qa:qb].unsqueeze(1).broadcast_to([128, U, m])
                    S.op("dve", lambda e, U=U, m=m, mk=mk, pTp_=pTp_: e.tensor_tensor(pTp_[:, 0:U * m].rearrange("p (u j) -> p u j", u=U),
                                                                          pTp_[:, 0:U * m].rearrange("p (u j) -> p u j", u=U), mk, ALU.mult),
                         reads=[kp_, "cp"], writes=[kp_])
                S.op("act", lambda e, psc=psc, U=U, m=m, qb=qb, pTc_=pTc_: e.activation(out=pTc_[0:qb, 0:U * m], in_=ps[psc][0:qb, 0:U * m], func=ACT.Exp, scale=SC),
                     reads=[("ps", psc)], writes=[kc2_])
                mk2 = mcur[0:qb, qa:qb].unsqueeze(1).broadcast_to([qb, U, m])
                S.op("dve", lambda e, U=U, m=m, qb=qb, mk2=mk2, pTc_=pTc_: e.tensor_tensor(pTc_[0:qb, 0:U * m].rearrange("p (u j) -> p u j", u=U),
                                                                             pTc_[0:qb, 0:U * m].rearrange("p (u j) -> p u j", u=U), mk2, ALU.mult),
                     reads=[kc2_, "cp"], writes=[kc2_])
                def pv_part(units=units, qa=qa, qb=qb, m=m, U=U, nb0=nb0, has_prev=has_prev, pTp_=pTp_, pTc_=pTc_, kp_=kp_, kc2_=kc2_,
                            g=g, dil=dil, bidx=bidx):
                    pn = pnext()
                    pd = pnext()
                    for u, (r, nb) in enumerate(units):
                        tn, td = [], []
                        if has_prev:
                            tn.append((vtt[:, bidx(r, nb - 1), :], pTp_[:, u * m:(u + 1) * m], ["vtt", kp_]))
                            td.append(((hmat if nb - 1 < 0 else ones), pTp_[:, u * m:(u + 1) * m], ["cp", kp_]))
                        tn.append((vtt[0:qb, bidx(r, nb), :], pTc_[0:qb, u * m:(u + 1) * m], ["vtt", kc2_]))
                        td.append(((hmat if nb < 0 else ones)[0:qb, :], pTc_[0:qb, u * m:(u + 1) * m], ["cp", kc2_]))
                        mmgroup(ps[pn][:, u * m:(u + 1) * m], ("ps", pn), tn)
                        mmgroup(ps[pd][:, u * m:(u + 1) * m], ("ps", pd), td)
                    if dil == 1:
                        base = 128 * nb0 + qa - l0
                        nv = numacc[:, base:base + U * m]
                        dv = denacc[:, base:base + U * m]
                        pnv, pdv = ps[pn][:, 0:U * m], ps[pd][:, 0:U * m]
                    else:
                        base = dil * (128 * nb0 + qa) - l0
                        r0 = units[0][0]
                        nv = numacc[:, base:base + dil * m].rearrange("p (j r) -> p r j", r=dil)[:, r0:r0 + U, :]
                        dv = denacc[:, base:base + dil * m].rearrange("p (j r) -> p r j", r=dil)[:, r0:r0 + U, :]
                        pnv = ps[pn][:, 0:U * m].rearrange("p (u j) -> p u j", u=U)
                        pdv = ps[pd][:, 0:U * m].rearrange("p (u j) -> p u j", u=U)
                    S.op("dve", lambda e, nv=nv, pnv=pnv: e.tensor_tensor(nv, pnv, nv, ALU.add), reads=[("ps", pn), *bk("pt2")], writes=[*bk("pt2")])
                    S.op("dve", lambda e, dv=dv, pdv=pdv: e.tensor_tensor(dv, pdv, dv, ALU.add), reads=[("ps", pd), *bk("ubufB")], writes=[*bk("ubufB")])
                fill.step()
                flush()
                if "pvdefer" in DBG:
                    defer(pv_part)
                else:
                    pv_part()
        flush()
        rden = denacc
        S.op("dve", lambda e: e.tensor_scalar(denacc, denacc, 1e-30, None, ALU.add), reads=[*bk("ubufB")], writes=[*bk("ubufB")])
        S.op("dve", lambda e: e.reciprocal(rden, denacc), reads=[*bk("ubufB")], writes=[*bk("ubufB")])
        S.op("pool", lambda e: e.tensor_tensor(attnT[:, hs, 0:n], numacc, rden, ALU.mult), reads=[*bk("pt2"), *bk("ubufB")], writes=[("aT", hs)])

    def merge_phase(l0, n, tiles):
        cyk = [("cy", c) for c in range(NC)]
        atk = [("aT", h) for h in range(HPG)]
        for j in range(NC):
            wco, kco = wget(w_co[:, j * 128:(j + 1) * 128], NC, 128)
            wgc, kgc = wget(w_in[:, 3 * D + 3 * AW + j * 128:3 * D + 3 * AW + (j + 1) * 128], NC, 128)
            wao, kao = wget(w_ao[:, j * 128:(j + 1) * 128], HPG, 128)
            wga, kga = wget(w_in[:, 4 * D + 3 * AW + j * 128:4 * D + 3 * AW + (j + 1) * 128], NC, 128)
            for (off, nn) in tiles:
                t = state["tog"]
                state["tog"] = (t + 1) % NSET
                ak = blk_keys("hT", off, nn)
                pbc = proj(wco, kco, convy, cyk, NC, off, nn)
                pgc = proj(wgc, kgc, hT, ak, NC, off, nn)
                pba = proj(wao, kao, attnT, atk, HPG, off, nn)
                pga = proj(wga, kga, hT, ak, NC, off, nn)
                S.op("act", lambda e, pgc=pgc, t=t, nn=nn: e.activation(out=t1b[t][:, 0:nn], in_=ps[pgc][:, 0:nn], func=ACT.Sigmoid),
                     reads=[("ps", pgc)], writes=[("t1", t)])
                S.op("act", lambda e, pga=pga, t=t, nn=nn: e.activation(out=t2b[t][:, 0:nn], in_=ps[pga][:, 0:nn], func=ACT.Sigmoid),
                     reads=[("ps", pga)], writes=[("t2", t)])
                S.op("dve", lambda e, pbc=pbc, t=t, nn=nn: e.tensor_tensor(t1b[t][:, 0:nn], ps[pbc][:, 0:nn], t1b[t][:, 0:nn], ALU.mult),
                     reads=[("ps", pbc), ("t1", t)], writes=[("t1", t)])
                S.op("dve", lambda e, pba=pba, t=t, nn=nn: e.tensor_tensor(t2b[t][:, 0:nn], ps[pba][:, 0:nn], t2b[t][:, 0:nn], ALU.mult),
                     reads=[("ps", pba), ("t2", t)], writes=[("t2", t)])
                S.op("pool", lambda e, j=j, t=t, off=off, nn=nn: e.tensor_tensor(r3v[:, j, off:off + nn], t1b[t][:, 0:nn], t2b[t][:, 0:nn], ALU.add),
                     reads=[("t1", t), ("t2", t)], writes=[("mg", j)])

    def tokmajor_accum(wd_fn, kcn, actT_, akeys, nblk):
        for ct in range(D // 256):
            wf, wk = wget(wd_fn(ct), kcn, 256)
            for b in range(nblk):
                pi = pnext()
                mmgroup(ps[pi][:, 0:256], ("ps", pi),
                        [(actT_[:, k, b * 128:(b + 1) * 128], wf(k), list(wk) + list(akeys)) for k in range(kcn)])
                dst = x1v[:, b, ct * 256:(ct + 1) * 256]
                S.op("dve", lambda e, dst=dst, pi=pi: e.tensor_tensor(dst, ps[pi][:, 0:256], dst, ALU.add),
                     reads=[("ps", pi), ("x1", b)], writes=[("x1", b)])

    def ffn_phase(l0, n, tiles, nblk):
        h2k = [("h2", b) for b in range(nblk)]
        for fb in range(FB):
            for fi in range(FCB):
                f = fb * FCB + fi
                if fi % 2 == 0:
                    npair = min(2, FCB - fi)
                    wg, kg = wget(w_up[:, f * 128:(f + npair) * 128], NC, npair * 128)
                    wu, ku = wget(w_up[:, DFF + f * 128:DFF + (f + npair) * 128], NC, npair * 128)
                jj = fi % 2
                g0, g1, g2 = [fconvw[:, f * 3 + k:f * 3 + k + 1] for k in range(3)]
                v0, v1, v2 = [fconvw[:, (FC + f) * 3 + k:(FC + f) * 3 + k + 1] for k in range(3)]
                S.op("pool", lambda e, f=f: e.tensor_copy(ubufA[:, 0:1], fcarry[:, f, 0:1]), reads=["fcarry"], writes=[*bk("ubufA")])
                S.op("pool", lambda e, f=f: e.tensor_copy(ptmp[:, 0:2], fcarry[:, f, 1:3]), reads=["fcarry"], writes=[*bk("ptmp")])
                S.op("pool", lambda e, f=f: e.tensor_copy(ubufB[:, 0:1], fcarry[:, FC + f, 0:1]), reads=["fcarry"], writes=[*bk("ubufB")])
                S.op("pool", lambda e, f=f: e.tensor_copy(pt2[:, 0:2], fcarry[:, FC + f, 1:3]), reads=["fcarry"], writes=[*bk("pt2")])
                for ti, (off, nn) in enumerate(tiles):
                    ak = blk_keys("h2", off, nn)
                    pg = proj(wg, kg, r3v, ak, NC, off, nn, j=jj)
                    pu = proj(wu, ku, r3v, ak, NC, off, nn, j=jj)
                    conv3(lambda pg=pg, nn=nn: ps[pg][:, 0:nn], [("ps", pg)], yA, ubufA, ptmp, "yA", "ubufA", "ptmp", g0, g1, g2, off, nn, ti)
                    conv3(lambda pu=pu, nn=nn: ps[pu][:, 0:nn], [("ps", pu)], yB, ubufB, pt2, "yB", "ubufB", "pt2", v0, v1, v2, off, nn, ti)
                S.op("dve", lambda e: e.tensor_tensor(yA[:, 0:n], yA[:, 0:n], ubufA[:, 0:n], ALU.add), reads=[*bk("yA"), *bk("ubufA")], writes=[*bk("yA")])
                S.op("dve", lambda e: e.tensor_tensor(yA[:, 0:n], yA[:, 0:n], ptmp[:, 0:n], ALU.add), reads=[*bk("yA"), *bk("ptmp")], writes=[*bk("yA")])
                S.op("pool", lambda e: e.tensor_tensor(yB[:, 0:n], yB[:, 0:n], ubufB[:, 0:n], ALU.add), reads=[*bk("yB"), *bk("ubufB")], writes=[*bk("yB")])
                S.op("pool", lambda e: e.tensor_tensor(yB[:, 0:n], yB[:, 0:n], pt2[:, 0:n], ALU.add), reads=[*bk("yB"), *bk("pt2")], writes=[*bk("yB")])
                S.op("pool", lambda e, f=f: e.tensor_copy(fcarry[:, f, 0:1], ubufA[:, n:n + 1]), reads=[*bk("ubufA")], writes=["fcarry"])
                S.op("pool", lambda e, f=f: e.tensor_copy(fcarry[:, f, 1:3], ptmp[:, n:n + 2]), reads=[*bk("ptmp")], writes=["fcarry"])
                S.op("pool", lambda e, f=f: e.tensor_copy(fcarry[:, FC + f, 0:1], ubufB[:, n:n + 1]), reads=[*bk("ubufB")], writes=["fcarry"])
                S.op("pool", lambda e, f=f: e.tensor_copy(fcarry[:, FC + f, 1:3], pt2[:, n:n + 2]), reads=[*bk("pt2")], writes=["fcarry"])
                S.op("act", lambda e: e.activation(out=yA[:, 0:n], in_=yA[:, 0:n], func=ACT.Silu), reads=[*bk("yA")], writes=[*bk("yA")])
                S.op("dve", lambda e, fi=fi: e.tensor_tensor(actT[:, fi, 0:n], yA[:, 0:n], yB[:, 0:n], ALU.mult),
                     reads=[*bk("yA"), *bk("yB")], writes=[("ac", fi)])
            tokmajor_accum(lambda ct, fb=fb: w_dn[fb * FCB * 128:(fb + 1) * FCB * 128, ct * 256:(ct + 1) * 256],
                           FCB, actT, [("ac", k) for k in range(FCB)], nblk)

    fz = sb("fz", [128, 1], F32)

    def fence(reads, writes):
        S.op("pool", lambda e: e.memset(fz[:], 0.0), reads=list(reads), writes=list(writes) + ["fz"])

    def load_x_block(l0):
        def f(b, xi):
            r0 = l0 + 2048 + 128 * b
            S.op("sp", lambda e: e.dma_start(out=xbs[xi][:], in_=xe[r0:r0 + 128, :]), writes=[("xb", xi)], dma=True)
        return f

    def zero_fill():
        S.op("dve", lambda e: e.memset(r3[:, 0:4096], 0.0), writes=[("kTt", 0)])
        zsrc = r3[:, 0:4096].rearrange("p (b d) -> p b d", b=4)[:, :, 0:HPG * 128]
        for g in (2, 1, 0):
            for h in range(HPG):
                gh = g * HPG + h
                S.op("pool", lambda e, gh=gh: e.dma_start(out=kcache[gh], in_=r3[:, 0:4096]), reads=[("kTt", 0)], writes=[("kc", gh)], dma=True)
            for b0 in range(0, 32, 4):
                S.op("pool", lambda e, g=g, b0=b0: e.dma_start(out=vcache[g, b0:b0 + 4].rearrange("b k d -> k b d"), in_=zsrc),
                     reads=[("kTt", 0)], writes=[("vc", g * HPG + h) for h in range(HPG)], dma=True)

    class _Stop(Exception):
        pass

    stg = {"n": 0}

    def stage(name):
        stg["n"] += 1
        if getattr(cfg, "max_stage", None) is not None and stg["n"] > cfg.max_stage:
            print("STOP before stage", stg["n"], name)
            raise _Stop()

    big_keys = [("hT", b) for b in range(NB)] + [("cy", c) for c in range(NC)]
    x1_keys = [("x1", b) for b in range(NB)]
    r3_attn = [("kTt", 0), ("kTt", 1), "vtt"]
    mg_keys = [("mg", j) for j in range(NC)]
    h2_keys = [("h2", b) for b in range(NB)]
    r4_attn = [("aT", h) for h in range(HPG)] + [("qT", i, g) for g in range(3) for i in range(2)]
    ac_keys = [("ac", k) for k in range(FCB)]
    out_keys = []

    def emit():
        hU = convy
        sts = list(cfg.halo_sts)
        nh = len(sts)
        bufs = [(hU, "hU") if (nh - i) % 2 == 1 else (hT, "hT") for i in range(nh)] + [(hT, "hT")]
        allst = sts + [cfg.main_sts[0]]

        def mk_gen(i):
            l0_, n_ = allst[i]
            return build_hT(lambda b, xi: xbs[xi][:], lambda b, xi: [("xb", xi)], mixg, bufs[i][0], bufs[i][1], n_ // 128,
                            load_fn=load_x_block(l0_))

        stage("halo first hT")
        zero_fill()
        run(mk_gen(0))
        for i, (l0, n) in enumerate(sts):
            stage("halo tables %d" % l0)
            build_tables(l0, n)
            stage("halo kv")
            bg = mk_gen(i + 1)
            kv_phase(l0, n, bufs[i][0], bufs[i][1], bg=bg)
            run(bg)
        for (l0, n) in cfg.main_sts:
            nblk = n // 128
            tiles = [(tl - l0, nn) for (tl, nn) in split_tiles(l0, l0 + n)]
            stage("main hT %d" % l0)
            if (l0, n) == cfg.main_sts[0]:
                fence(x1_keys + [("hU", b) for b in range(NB)], [("cy", c) for c in range(NC)])
            else:
                fence(x1_keys, big_keys)
                run(build_hT(lambda b, xi: xbs[xi][:], lambda b, xi: [("xb", xi)], mixg, hT, "hT", nblk, load_fn=load_x_block(l0)))
            build_tables(l0, n)
            stage("main kv")
            kv_phase(l0, n, hT, "hT")
            stage("main conv+attn")
            fence(h2_keys + mg_keys, r3_attn)
            fence(ac_keys, r4_attn)
            cg = conv_phase(l0, n, tiles)
            qi = 0
            run(q_gen(l0, n, tiles, 0, qi))
            for hs in range(HPG):
                nxt = q_gen(l0, n, tiles, hs + 1, 1 - qi) if hs + 1 < HPG else None
                attn_phase(l0, n, tiles, hs, qi, _Fill([cg, nxt]))
                if nxt is not None:
                    run(nxt)
                qi = 1 - qi
            run(cg)
            stage("main merge")
            fence(r3_attn, mg_keys)
            merge_phase(l0, n, tiles)
            stage("main mergeout")
            fence(big_keys, x1_keys)
            for b in range(nblk):
                r0 = l0 + 2048 + 128 * b
                S.op("sp", lambda e, b=b, r0=r0: e.dma_start(out=x1v[:, b, :], in_=xe[r0:r0 + 128, :]), writes=[("x1", b)], dma=True)
            tokmajor_accum(lambda ct: w_mo[:, ct * 256:(ct + 1) * 256], NC, r3v, mg_keys, nblk)
            stage("main h2")
            fence(mg_keys, h2_keys)
            run(build_hT(lambda b, xi: x1v[:, b, :], lambda b, xi: [("x1", b)], ffng, r3v, "h2", nblk))
            stage("main ffn")
            fence(r4_attn, ac_keys)
            ffn_phase(l0, n, tiles, nblk)
            stage("main out")
            for b in range(nblk):
                l = l0 + 128 * b
                if l < 0:
                    continue
                S.op("sp", lambda e, b=b, l=l: e.dma_start(out=out[l:l + 128, :], in_=x1v[:, b, :]), reads=[("x1", b)], writes=[("out", l)], dma=True)
                out_keys.append(("out", l))

    try:
        emit()
    except _Stop:
        pass
    S.op("sp", lambda e: None, reads=out_keys)
    S.build()
    return nc, S


def host_consts(cfg, is_first_half):
    f32 = np.float32
    ident = np.eye(128, dtype=f32)
    RTm = np.zeros((128, 128), f32)
    for m in range(16):
        RTm[m + 16, m] = -1.0
        RTm[m, m + 16] = 1.0
    kk = np.arange(128)[:, None]
    qq = np.arange(128)[None, :]
    mprev = (kk >= qq).astype(f32)
    mcur = (kk <= qq).astype(f32)
    ones = np.ones((128, 128), f32)
    hmat = np.zeros((128, 128), f32) if is_first_half else ones
    return np.concatenate([ident, RTm, mprev, mcur, ones, hmat], axis=1)


def host_smallp(cfg, mix_norm, ffn_norm, conv_mix_w, ffn_conv_w, q_norm, k_norm):
    NC, FC = cfg.NC, cfg.FC
    sp = np.zeros((128, cfg.NSP), np.float32)
    sp[:, 0:NC] = mix_norm.reshape(NC, 128).T
    sp[:, NC:2 * NC] = ffn_norm.reshape(NC, 128).T
    sp[:, 2 * NC:5 * NC] = conv_mix_w.reshape(3, NC, 128).transpose(2, 1, 0).reshape(128, NC * 3)
    sp[:, 5 * NC:5 * NC + 6 * FC] = ffn_conv_w.reshape(3, 2 * FC, 128).transpose(2, 1, 0).reshape(128, 2 * FC * 3)
    o0 = 5 * NC + 6 * FC
    sp[:, o0] = q_norm
    sp[:, o0 + 1] = k_norm
    half = 16
    invf = np.zeros(128, np.float32)
    fr = (np.float32(500000.0) ** (-np.arange(half, dtype=np.float32) * np.float32(2.0 / 32))).astype(np.float32)
    invf[0:16] = fr
    invf[16:32] = fr
    sp[:, o0 + 2] = invf
    return sp


def make_in_maps(cfg, x, positions, mix_norm, w_in, conv_mix_w, w_conv_out, q_norm, k_norm, w_attn_out,
                 w_merge_out, ffn_norm, w_up, ffn_conv_w, w_down):
    B = x.shape[0]
    D = cfg.D
    sp = host_smallp(cfg, mix_norm[0], ffn_norm[0], conv_mix_w[0], ffn_conv_w[0], q_norm[0], k_norm[0])
    shared = {
        "w_in": np.ascontiguousarray(w_in[0]), "w_co": np.ascontiguousarray(w_conv_out[0]),
        "w_ao": np.ascontiguousarray(w_attn_out[0]), "w_mo": np.ascontiguousarray(w_merge_out[0]),
        "w_up": np.ascontiguousarray(w_up[0]), "w_dn": np.ascontiguousarray(w_down[0]), "smallp": sp,
    }
    maps = []
    for core in range(2 * B):
        b, half = core // 2, core % 2
        T0 = 2048 * half
        xe = np.zeros((4096, D), np.float32)
        pe = np.zeros((1, 4096), np.int32)
        lo = T0 - 2048
        if lo >= 0:
            xe[:] = x[b, lo:lo + 4096]
            pe[0] = positions[b, lo:lo + 4096]
        else:
            xe[2048:] = x[b, 0:2048]
            pe[0, 2048:] = positions[b, 0:2048]
        m = dict(shared)
        m["xe"] = xe
        m["pos"] = pe
        m["cpack"] = host_consts(cfg, half == 0)
        maps.append(m)
    return maps


_CACHE = {}


def kernel(x, positions, mix_norm, w_in, conv_mix_w, w_conv_out, q_norm, k_norm, w_attn_out,
           w_merge_out, ffn_norm, w_up, ffn_conv_w, w_down):
    cfg = Cfg()
    args = [np.asarray(a) for a in (x, positions, mix_norm, w_in, conv_mix_w, w_conv_out, q_norm, k_norm,
                                    w_attn_out, w_merge_out, ffn_norm, w_up, ffn_conv_w, w_down)]
    maps = make_in_maps(cfg, *args)
    if "nc" not in _CACHE:
        _CACHE["nc"] = build_program(cfg)[0]
    nc = _CACHE["nc"]
    res = run_bass_kernel_spmd(nc, maps, core_ids=list(range(8)))
    B = args[0].shape[0]
    outp = np.zeros((B, 4096, cfg.D), np.float32)
    for core in range(8):
        b, half = core // 2, core % 2
        outp[b, 2048 * half:2048 * (half + 1)] = res.results[core]["out"]
    return outp
```

```python
import math
from contextlib import ExitStack

import numpy as np
import concourse.bass as bass
import concourse.mybir as mybir
from concourse.bass_utils import run_bass_kernel_spmd

ACT = mybir.ActivationFunctionType
ALU = mybir.AluOpType
F32 = mybir.dt.float32
BF16 = mybir.dt.bfloat16
I32 = mybir.dt.int32


class Sched:
    BLOCK_ATTR = {"pe": "tensor", "act": "scalar", "dve": "vector", "pool": "gpsimd", "sp": "sync"}

    def __init__(self, nc, n_dma_sems=8):
        self.nc = nc
        self.ops = []
        self.n_dma_sems = n_dma_sems
        self.last_touch = {}
        self.inorder = ("pe",)

    def op(self, eng, fn, reads=(), writes=(), dma=False, pos=None):
        o = {"eng": eng, "fn": fn, "reads": tuple(reads), "writes": tuple(writes), "dma": dma,
             "pos": (len(self.ops) if pos is None else pos - 0.5), "seq": len(self.ops)}
        if pos is None:
            for k in o["reads"] + o["writes"]:
                self.last_touch[k] = len(self.ops)
        self.ops.append(o)

    def build(self):
        nc = self.nc
        ops = sorted(self.ops, key=lambda o: (o["pos"], o["seq"]))
        last_writer = {}
        readers = {}
        for i, o in enumerate(ops):
            deps = set()
            for k in o["reads"]:
                if k in last_writer:
                    deps.add(last_writer[k])
                if isinstance(k, tuple) and k[0] == "ps":
                    for r in readers.get(k, ()):
                        if ops[r]["eng"] != o["eng"]:
                            deps.add(r)
            for k in o["writes"]:
                if k in last_writer:
                    deps.add(last_writer[k])
                for r in readers.get(k, ()):
                    deps.add(r)
            deps.discard(i)
            if o["eng"] in self.inorder:
                deps = {d for d in deps if ops[d]["eng"] != o["eng"] or ops[d]["dma"] != o["dma"]}
            o["deps"] = deps
            for k in o["reads"]:
                readers.setdefault(k, []).append(i)
            for k in o["writes"]:
                last_writer[k] = i
                readers[k] = []
        last_of = {}
        for i, o in enumerate(ops[:-1]):
            last_of[o["eng"]] = i
        for e_, i in last_of.items():
            if i != len(ops) - 1:
                ops[-1]["deps"].add(i)
        for o in ops:
            o["signal"] = o["dma"]
        for o in ops:
            for d in o["deps"]:
                ops[d]["signal"] = True
        with ExitStack() as es:
            esem = {e: es.enter_context(nc.semaphore("s_" + e)) for e in ("pe", "act", "dve", "pool")}
            dsem = {q: [es.enter_context(nc.semaphore("d_%s%d" % (q, j))) for j in range(self.n_dma_sems)]
                    for q in ("sp", "pool")}
            cnt = {e: 0 for e in esem}
            dcnt = {q: 0 for q in dsem}
            for o in ops:
                if o["dma"]:
                    q = o["eng"]
                    j = dcnt[q] % self.n_dma_sems
                    o["sem"] = dsem[q][j]
                    o["val"] = 16 * (dcnt[q] // self.n_dma_sems + 1)
                    o["prev"] = o["val"] - 16
                    dcnt[q] += 1
                elif o["signal"]:
                    cnt[o["eng"]] += 1
                    o["sem"] = esem[o["eng"]]
                    o["val"] = cnt[o["eng"]]
            by_eng = {}
            for i, o in enumerate(ops):
                by_eng.setdefault(o["eng"], []).append(i)
            self.stats = {e: len(v) for e, v in by_eng.items()}
            nwaits = [0]

            def make_body(eng):
                def body(e):
                    waited = {}
                    for i in by_eng.get(eng, ()):
                        o = ops[i]
                        need = {}
                        for d in o["deps"]:
                            p = ops[d]
                            key = id(p["sem"])
                            if need.get(key, (None, 0))[1] < p["val"]:
                                need[key] = (p["sem"], p["val"])
                        if o["dma"] and o["prev"] > 0:
                            key = id(o["sem"])
                            if need.get(key, (None, 0))[1] < o["prev"]:
                                need[key] = (o["sem"], o["prev"])
                        for key, (s, v) in need.items():
                            if waited.get(key, 0) >= v:
                                continue
                            e.wait_ge(s, v)
                            waited[key] = v
                            nwaits[0] += 1
                        ins = o["fn"](e)
                        if o["dma"]:
                            ins.then_inc(o["sem"], 16)
                        elif o["signal"]:
                            ins.then_inc(o["sem"], 1)
                return body

            with nc.Block() as block:
                for eng, attr in self.BLOCK_ATTR.items():
                    getattr(block, attr)(make_body(eng))
            self.stats["waits"] = nwaits[0]
        return nc


class Cfg:
    def __init__(self, D=2048, HPG=8, DFF=5632, FB=4):
        self.D, self.HPG, self.DFF, self.FB = D, HPG, DFF, FB
        self.NC = D // 128
        self.AW = 3 * HPG * 128
        self.AO = HPG * 128
        self.FC = DFF // 128
        self.FCB = self.FC // FB
        assert self.FCB * FB == self.FC
        self.IN_COLS = 3 * D + 3 * self.AW + 2 * D
        self.NT = 768
        self.halo_sts = [(-2048, 768), (-1280, 768), (-512, 384)]
        self.main_sts = [(-128, 768), (640, 768), (1408, 640)]
        self.NSP = 5 * self.NC + 6 * self.FC + 3


DILS = (1, 4, 16)
KV_LO = (-256, -1024, -2048)
NBMIN = (-2, -2, -1)


def split_tiles(lo, hi):
    n = hi - lo
    k = -(-n // 384)
    nb = n // 128
    out = []
    o = lo
    for i in range(k):
        s = (nb // k + (1 if i < nb % k else 0)) * 128
        out.append((o, s))
        o += s
    return out


def build_program(cfg):
    D, NC, HPG, DFF, FC, FCB, FB, NT = cfg.D, cfg.NC, cfg.HPG, cfg.DFF, cfg.FC, cfg.FCB, cfg.FB, cfg.NT
    AW, AO = cfg.AW, cfg.AO
    NB = NT // 128
    nc = bass.Bass("TRN2", target_bir_lowering=False)

    def din(name, shape, dt=F32):
        return nc.dram_tensor(name, shape, dt, kind="ExternalInput").ap()

    xe = din("xe", [4096, D])
    pos = din("pos", [1, 4096], I32)
    w_in = din("w_in", [D, cfg.IN_COLS])
    w_co = din("w_co", [D, D])
    w_ao = din("w_ao", [AO, D])
    w_mo = din("w_mo", [D, D])
    w_up = din("w_up", [D, 2 * DFF])
    w_dn = din("w_dn", [DFF, D])
    smallp = din("smallp", [128, cfg.NSP])
    cpack = din("cpack", [128, 6 * 128])
    out = nc.dram_tensor("out", [2048, D], F32, kind="ExternalOutput").ap()
    kcache = nc.dram_tensor("kcache", [3 * HPG, 128, 4096], BF16).ap()
    vcache = nc.dram_tensor("vcache", [3, 32, 128, HPG * 128], BF16).ap()

    sb = nc.alloc_sbuf_tensor
    big = sb("big", [128, NB * D], F32)
    x1v = big[:].rearrange("p (b f) -> p b f", b=NB)
    bigbf = big[:].bitcast(BF16).rearrange("p (a c t) -> p a c t", a=2, c=NC)
    hT = bigbf[:, 0]
    convy = bigbf[:, 1]
    R3N = max(NC * NT, 8192)
    r3 = sb("r3", [128, R3N], BF16)
    r3v = r3[:, 0:NC * NT].rearrange("p (c t) -> p c t", c=NC)
    kTts = [r3[:, 0:4096], r3[:, 8192:12288]] if R3N >= 12288 else [r3[:, 0:4096], r3[:, 0:4096]]
    KDB = 2 if R3N >= 12288 else 1
    vtt = r3[:, 4096:8192].rearrange("p (b d) -> p b d", b=32)
    R4N = max(HPG * NT + 6 * NT, FCB * NT)
    r4 = sb("r4", [128, R4N], BF16)
    attnT = r4[:, 0:HPG * NT].rearrange("p (h t) -> p h t", h=HPG)
    qTs = [r4[:, HPG * NT + i * 3 * NT:HPG * NT + (i + 1) * 3 * NT].rearrange("p (g t) -> p g t", g=3) for i in range(2)]
    actT = r4[:, 0:FCB * NT].rearrange("p (k t) -> p k t", k=FCB)
    KCM = max(NC, FCB, HPG)
    NSLOT = 8
    wsl = sb("wsl", [128, KCM, NSLOT * 128], BF16)
    cp = sb("cp", [128, 6 * 128], BF16)
    ident, RTm, mprev, mcur, ones, hmat = [cp[:, i * 128:(i + 1) * 128] for i in range(6)]
    sp_sb = sb("sp_sb", [128, cfg.NSP], F32)
    mixg = sp_sb[:, 0:NC]
    ffng = sp_sb[:, NC:2 * NC]
    convw = sp_sb[:, 2 * NC:5 * NC]
    fconvw = sp_sb[:, 5 * NC:5 * NC + 6 * FC]
    o0 = 5 * NC + 6 * FC
    gq = sp_sb[:, o0:o0 + 1]
    gk = sp_sb[:, o0 + 1:o0 + 2]
    invf = sp_sb[:, o0 + 2:o0 + 3]
    eps = sb("eps", [128, 1], F32)
    Ctab = sb("Ctab", [128, NT], F32)
    Stab = sb("Stab", [128, NT], F32)
    xbs = [sb("xb%d" % i, [128, D], F32) for i in range(2)]
    hbs = [sb("hb%d" % i, [128, D], BF16) for i in range(2)]
    ss = sb("ss", [128, 2], F32)
    ln1 = sb("ln1", [128, 2], F32)
    rs1 = sb("rs1", [128, 2], F32)
    TW = 384
    NSET = 3
    ybuf = [sb("ybuf%d" % i, [128, TW], BF16) for i in range(NSET)]
    sqbuf = [sb("sqbuf%d" % i, [128, TW], BF16) for i in range(NSET)]
    lnv = [sb("lnv%d" % i, [128, TW], F32) for i in range(NSET)]
    t1b = [sb("t1b%d" % i, [128, TW], F32) for i in range(NSET)]
    t2b = [sb("t2b%d" % i, [128, TW], F32) for i in range(NSET)]
    ubufA = sb("ubufA", [128, NT + 2], F32)
    ubufB = sb("ubufB", [128, NT + 2], F32)
    yA = sb("yA", [128, NT + 2], F32)
    yB = sb("yB", [128, NT + 2], F32)
    cxs = sb("cxs", [128, TW], F32)
    ptmp = sb("ptmp", [128, NT + 2], F32)
    pt2 = sb("pt2", [128, NT + 2], F32)
    ksts = [sb("kst%d" % i, [128, NT], BF16) for i in range(2)]
    vTs = sb("vTs", [128, NT], BF16)
    vst = [sb("vst%d" % i, [128, 8, 128], BF16) for i in range(2)]
    pTp = sb("pTp", [128, 512], BF16)
    pTc = sb("pTc", [128, 512], BF16)
    pTps = [pTp[:, :], vst[0][:, 0:4, :].rearrange("p a d -> p (a d)")]
    pTcs = [pTc[:, :], vst[1][:, 0:4, :].rearrange("p a d -> p (a d)")]
    ucarry = sb("ucarry", [128, NC, 3], F32)
    fcarry = sb("fcarry", [128, 2 * FC, 3], F32)
    ps = [nc.alloc_psum_tensor("ps%d" % i, [128, 512], F32) for i in range(8)]

    S = Sched(nc)
    S.inorder = getattr(cfg, "inorder", ("pe", "dve"))
    DBG = getattr(cfg, "dbg", ())
    state = {"p": 0, "wi": 0, "anchor": [], "tog": 0, "vs": 0, "xi": 0, "qi": 0, "ki": 0, "ksi": 0, "pti": 0, "pending": []}

    def bk(name):
        return [(name, 0), (name, 1)]

    def pnext():
        i = state["p"]
        state["p"] = (i + 1) % 8
        return i

    def psbf(i):
        return ps[i][:].bitcast(BF16).rearrange("p (u d) -> p u d", u=8)

    PD = NSLOT - 2

    def wget(dram_ap, kc, ncols):
        ns = ncols // 128
        i = state["wi"]
        if (i % NSLOT) + ns > NSLOT:
            i += NSLOT - (i % NSLOT)
        state["wi"] = i + ns
        s = i % NSLOT
        keys = [("w", s + j) for j in range(ns)]
        anc = state["anchor"]
        posn = anc[len(anc) - PD] if len(anc) >= PD else 0
        for kk_ in keys:
            posn = max(posn, S.last_touch.get(kk_, -1) + 1)
        anc.append(len(S.ops))
        dst = wsl[:, 0:kc, s * 128:(s + ns) * 128]
        src = dram_ap.rearrange("(c p) n -> p c n", p=128)
        S.op("pool", lambda e, dst=dst, src=src: e.dma_start(out=dst, in_=src), writes=keys, dma=True, pos=posn)

        def wf(k, j=None):
            if j is None:
                return wsl[:, k, s * 128:(s + ns) * 128]
            return wsl[:, k, (s + j) * 128:(s + j + 1) * 128]

        return wf, keys

    def mmgroup(out_ap, out_key, terms):
        n = len(terms)
        for i, (l, r, keys) in enumerate(terms):
            S.op("pe", lambda e, l=l, r=r, i=i: e.matmul(out_ap, l, r, start=(i == 0), stop=(i == n - 1)),
                 reads=keys, writes=[out_key])

    def proj(wf, wkeys, actT_, akeys, kcn, off, nn, j=None):
        pi = pnext()
        mmgroup(ps[pi][:, 0:nn], ("ps", pi),
                [(wf(k, j), actT_[:, k, off:off + nn], list(wkeys) + list(akeys)) for k in range(kcn)])
        flush()
        return pi

    def blk_keys(name, off, nn):
        return [(name, b) for b in range(off // 128, (off + nn) // 128)]

    S.op("pool", lambda e: e.dma_start(out=cp[:], in_=cpack), writes=["cp"], dma=True)
    S.op("sp", lambda e: e.dma_start(out=sp_sb[:], in_=smallp), writes=["spb"], dma=True)
    S.op("dve", lambda e: e.memset(eps[:], 1e-6), writes=["eps"])
    S.op("dve", lambda e: e.memset(ucarry[:], 0.0), writes=["ucarry"])
    S.op("dve", lambda e: e.memset(fcarry[:], 0.0), writes=["fcarry"])

    def build_hT(src_fn, src_keys_fn, gcols, dstT, dkey, nblk, load_fn=None):
        for b in range(nblk):
            xi = state["xi"]
            state["xi"] = 1 - xi
            hb = hbs[xi]
            hk = ("hb", xi)
            if load_fn is not None:
                load_fn(b, xi)
            src = src_fn(b, xi)
            sk = src_keys_fn(b, xi)
            ssx, ln1x, rs1x = ss[:, xi:xi + 1], ln1[:, xi:xi + 1], rs1[:, xi:xi + 1]
            S.op("dve", lambda e, ssx=ssx: e.memset(ssx, 0.0), writes=[("ss", xi)])
            S.op("act", lambda e, src=src, hb=hb, ssx=ssx: e.activation(out=hb[:], in_=src, func=ACT.Square, accum_out=ssx),
                 reads=sk + [("ss", xi)], writes=[hk, ("ss", xi)])
            S.op("act", lambda e, ssx=ssx, ln1x=ln1x: e.activation(out=ln1x, in_=ssx, func=ACT.Ln, scale=1.0 / D, bias=eps[:, 0:1]),
                 reads=[("ss", xi), "eps"], writes=[("ln1", xi)])
            S.op("act", lambda e, ln1x=ln1x, rs1x=rs1x: e.activation(out=rs1x, in_=ln1x, func=ACT.Exp, scale=-0.5),
                 reads=[("ln1", xi)], writes=[("rs1", xi)])
            S.op("dve", lambda e, src=src, hb=hb, rs1x=rs1x: e.tensor_scalar(hb[:], src, rs1x, None, ALU.mult),
                 reads=sk + [("rs1", xi)], writes=[hk])
            for c0 in range(0, NC, 8):
                pi = pnext()
                pb = psbf(pi)
                cn = min(8, NC - c0)
                for c in range(c0, c0 + cn):
                    S.op("pe", lambda e, c=c, pb=pb, c0=c0, hb=hb: e.transpose(pb[:, c - c0, :], hb[:, c * 128:(c + 1) * 128], ident),
                         reads=[hk, "cp"], writes=[("ps", pi)])
                for c in range(c0, c0 + cn):
                    dst = dstT[:, c, b * 128:(b + 1) * 128]
                    if c % 2 == 0:
                        S.op("act", lambda e, dst=dst, pb=pb, c=c, c0=c0: e.activation(
                            out=dst, in_=pb[:, c - c0, :], func=ACT.Copy, scale=gcols[:, c:c + 1]),
                            reads=[("ps", pi), "spb"], writes=[(dkey, b)])
                    else:
                        S.op("dve", lambda e, dst=dst, pb=pb, c=c, c0=c0: e.tensor_scalar(
                            dst, pb[:, c - c0, :], gcols[:, c:c + 1], None, ALU.mult),
                            reads=[("ps", pi), "spb"], writes=[(dkey, b)])
            yield b

    def run(gen):
        for _ in gen:
            pass

    def build_tables(l0, n):
        posi = ubufB[:, 0:n].bitcast(I32)
        ki = ubufA[:, 0:n].bitcast(I32)
        ang = yA[:, 0:n]
        kf = yB[:, 0:n]
        TWO_PI = 2.0 * math.pi
        C1 = 6.28125
        C2 = TWO_PI - C1
        S.op("sp", lambda e: e.dma_start(out=posi, in_=pos[0:1, l0 + 2048:l0 + 2048 + n].broadcast_to([128, n])),
             writes=[*bk("ubufB")], dma=True)
        for which, tab in ((0, Stab), (1, Ctab)):
            S.op("dve", lambda e: e.tensor_copy(ang, posi), reads=[*bk("ubufB")], writes=[*bk("yA")])
            if which == 0:
                S.op("dve", lambda e: e.tensor_scalar(ang, ang, invf, None, ALU.mult), reads=[*bk("yA"), "spb"], writes=[*bk("yA")])
            else:
                S.op("dve", lambda e: e.tensor_scalar(ang, ang, invf, 0.5 * math.pi, ALU.mult, ALU.add),
                     reads=[*bk("yA"), "spb"], writes=[*bk("yA")])
            S.op("dve", lambda e: e.tensor_scalar(kf, ang, 1.0 / TWO_PI, None, ALU.mult), reads=[*bk("yA")], writes=[*bk("yB")])
            S.op("dve", lambda e: e.tensor_copy(ki, kf), reads=[*bk("yB")], writes=[*bk("ubufA")])
            S.op("dve", lambda e: e.tensor_copy(kf, ki), reads=[*bk("ubufA")], writes=[*bk("yB")])
            S.op("dve", lambda e: e.scalar_tensor_tensor(ang, kf, -C1, ang, ALU.mult, ALU.add), reads=[*bk("yB"), *bk("yA")], writes=[*bk("yA")])
            S.op("dve", lambda e: e.scalar_tensor_tensor(ang, kf, -C2, ang, ALU.mult, ALU.add), reads=[*bk("yB"), *bk("yA")], writes=[*bk("yA")])
            S.op("dve", lambda e: e.tensor_scalar(kf, ang, math.pi, -TWO_PI, ALU.is_gt, ALU.mult), reads=[*bk("yA")], writes=[*bk("yB")])
            S.op("dve", lambda e: e.tensor_tensor(ang, ang, kf, ALU.add), reads=[*bk("yA"), *bk("yB")], writes=[*bk("yA")])
            S.op("dve", lambda e: e.tensor_scalar(kf, ang, -math.pi, TWO_PI, ALU.is_lt, ALU.mult), reads=[*bk("yA")], writes=[*bk("yB")])
            S.op("dve", lambda e: e.tensor_tensor(ang, ang, kf, ALU.add), reads=[*bk("yA"), *bk("yB")], writes=[*bk("yA")])
            S.op("dve", lambda e: e.tensor_scalar(ang, ang, 3.1415925, -3.1415925, ALU.min, ALU.max), reads=[*bk("yA")], writes=[*bk("yA")])
            S.op("act", lambda e, tab=tab: e.activation(out=tab[:, 0:n], in_=ang, func=ACT.Sin),
                 reads=[*bk("yA")], writes=["tab%d" % which])

    def defer(fn):
        state["pending"].append(fn)

    def flush():
        while state["pending"]:
            state["pending"].pop(0)()

    def normrope(pi, nn, off, gcol, dest, dkeys, dil):
        t = state["tog"]
        state["tog"] = (t + 1) % NSET
        xq = ps[pi][:, 0:nn]
        yb_, sq_, ln_, t1_, t2_ = ybuf[t][:, 0:nn], sqbuf[t][:, 0:nn], lnv[t][:, 0:nn], t1b[t][:, 0:nn], t2b[t][:, 0:nn]
        rs_ = ln_
        S.op("act", lambda e: e.activation(out=yb_, in_=xq, func=ACT.Copy, scale=gcol), reads=[("ps", pi), "spb"], writes=[("yb", t)])
        S.op("act", lambda e: e.activation(out=sq_, in_=xq, func=ACT.Square), reads=[("ps", pi)], writes=[("sq", t)])
        S.op("dve", lambda e: e.scalar_tensor_tensor(t1_, xq, gcol, Ctab[:, off:off + nn], ALU.mult, ALU.mult),
             reads=[("ps", pi), "spb", "tab1"], writes=[("t1", t)])

        def part_b():
            pj = pnext()
            S.op("pe", lambda e: e.matmul(ps[pj][:, 0:nn], ones, sq_, start=True, stop=True), reads=[("sq", t), "cp"], writes=[("ps", pj)])
            pk = pnext()
            S.op("pe", lambda e: e.matmul(ps[pk][:, 0:nn], RTm, yb_, start=True, stop=True), reads=[("yb", t), "cp"], writes=[("ps", pk)])
            S.op("act", lambda e: e.activation(out=ln_, in_=ps[pj][:, 0:nn], func=ACT.Ln, scale=1.0 / 128, bias=eps[:, 0:1]),
                 reads=[("ps", pj), "eps"], writes=[("ln", t)])
            S.op("act", lambda e: e.activation(out=rs_, in_=ln_, func=ACT.Exp, scale=-0.5), reads=[("ln", t)], writes=[("ln", t)])
            S.op("dve", lambda e: e.tensor_tensor(t2_, ps[pk][:, 0:nn], Stab[:, off:off + nn], ALU.mult),
                 reads=[("ps", pk), "tab0"], writes=[("t2", t)])
            S.op("dve", lambda e: e.tensor_tensor(t1_, t1_, t2_, ALU.add), reads=[("t1", t), ("t2", t)], writes=[("t1", t)])
            if dil == 1:
                a_, b_ = t1_, rs_
            else:
                a_ = t1_.rearrange("p (i r) -> p r i", r=dil)
                b_ = rs_.rearrange("p (i r) -> p r i", r=dil)
            S.op("dve", lambda e: e.tensor_tensor(dest, a_, b_, ALU.mult), reads=[("t1", t), ("ln", t)], writes=dkeys)

        defer(part_b)

    def kv_phase(l0, n, hTb, hkey, bg=None):
        for g in range(3):
            dil = DILS[g]
            lo = max(l0, KV_LO[g])
            hi = l0 + n
            if lo >= hi:
                continue
            tiles = split_tiles(lo, hi)
            ng = hi - lo
            i0, i1 = lo // dil, hi // dil
            for h in range(HPG):
                gh = g * HPG + h
                ksi = state["ksi"]
                state["ksi"] = 1 - ksi
                kst = ksts[ksi]
                kkey_ = ("kst", ksi)
                if h % 2 == 0:
                    np_ = min(2, HPG - h)
                    wfK, wkK = wget(w_in[:, 3 * D + AW + gh * 128:3 * D + AW + (gh + np_) * 128], NC, np_ * 128)
                    wfV, wkV = wget(w_in[:, 3 * D + 2 * AW + gh * 128:3 * D + 2 * AW + (gh + np_) * 128], NC, np_ * 128)
                jh = h % 2
                wf, wk = wfK, wkK
                kview = kst[:, 0:ng] if dil == 1 else kst[:, 0:ng].rearrange("p (r i) -> p r i", r=dil)
                for (tl, nn) in tiles:
                    off = tl - l0
                    pi = proj(wf, wk, hTb, blk_keys(hkey, off, nn), NC, off, nn, j=jh)
                    o2 = (tl - lo) // dil
                    dest = kview[:, o2:o2 + nn] if dil == 1 else kview[:, :, o2:o2 + nn // dil]
                    normrope(pi, nn, off, gk, dest, [kkey_], dil)
                kc_v = kcache[gh].rearrange("d (r i) -> d r i", r=dil)[:, :, i0 + 2048 // dil:i1 + 2048 // dil]
                ksrc = kst[:, 0:ng].rearrange("p (r i) -> p r i", r=dil)
                def k_dma(kc_v=kc_v, ksrc=ksrc, kkey_=kkey_, gh=gh):
                    S.op("sp", lambda e: e.dma_start(out=kc_v, in_=ksrc), reads=[kkey_], writes=[("kc", gh)], dma=True)
                defer(k_dma)
                if "nov" in DBG:
                    continue
                if bg is not None:
                    next(bg, None)
                wf, wk = wfV, wkV
                for (tl, nn) in tiles:
                    off = tl - l0
                    pi = proj(wf, wk, hTb, blk_keys(hkey, off, nn), NC, off, nn, j=jh)
                    S.op("act", lambda e, pi=pi, off=off, nn=nn: e.activation(out=vTs[:, off:off + nn], in_=ps[pi][:, 0:nn], func=ACT.Copy),
                         reads=[("ps", pi)], writes=["vTs"])
                def v_part(g=g, h=h, gh=gh, dil=dil, i0=i0, i1=i1):
                    NBR = 32 // dil
                    nbc = -16 // dil
                    for nb in range(i0 // 128, (i1 - 1) // 128 + 1):
                        ka = max(i0, 128 * nb) - 128 * nb
                        kb = min(i1, 128 * nb + 128) - 128 * nb
                        m = kb - ka
                        units = [(r, nb) for r in range(dil)]
                        for u0 in range(0, len(units), 8):
                            us = units[u0:u0 + 8]
                            U = len(us)
                            pi = pnext()
                            pb = psbf(pi)
                            for u, (r, nb_) in enumerate(us):
                                c0 = dil * (128 * nb_ + ka) + r - l0
                                src = vTs[:, c0:c0 + dil * (m - 1) + 1:dil]
                                S.op("pe", lambda e, pb=pb, u=u, src=src, m=m: e.transpose(pb[0:m, u, :], src, ident),
                                     reads=["vTs", "cp"], writes=[("ps", pi)])
                            vs = state["vs"]
                            state["vs"] = 1 - vs
                            S.op("dve", lambda e, pb=pb, vs=vs, m=m, U=U: e.tensor_copy(vst[vs][0:m, 0:U, :], pb[0:m, 0:U, :]),
                                 reads=[("ps", pi)], writes=[("vst", vs)])
                            b0 = us[0][0] * NBR + (nb - nbc)
                            dst = vcache[g, b0:b0 + (U - 1) * NBR + 1:NBR, ka:kb, h * 128:(h + 1) * 128].rearrange("u k d -> k u d")
                            if "novdma" not in DBG:
                                S.op("sp", lambda e, dst=dst, vs=vs, m=m, U=U: e.dma_start(out=dst, in_=vst[vs][0:m, 0:U, :]),
                                     reads=[("vst", vs)], writes=[("vc", gh)], dma=True)
                defer(v_part)
        flush()


    def conv3(src_fn, skeys, A2, A1, A0, kA2, kA1, kA0, w0, w1, w2, off, nn, ti):
        S.op("act", lambda e: e.activation(out=A2[:, off:off + nn], in_=src_fn(), func=ACT.Copy, scale=w2), reads=skeys + ["spb"], writes=[(kA2, ti)])
        S.op("act", lambda e: e.activation(out=A1[:, off + 1:off + 1 + nn], in_=src_fn(), func=ACT.Copy, scale=w1), reads=skeys + ["spb"], writes=[(kA1, ti)])
        S.op("dve", lambda e: e.tensor_scalar(A0[:, off + 2:off + 2 + nn], src_fn(), w0, None, ALU.mult), reads=skeys + ["spb"], writes=[(kA0, ti)])

    def conv_phase(l0, n, tiles):
        for c in range(NC):
            wfb, wkb = wget(w_in[:, c * 128:(c + 1) * 128], NC, 128)
            wfc, wkc = wget(w_in[:, D + c * 128:D + (c + 1) * 128], NC, 128)
            wfx, wkx = wget(w_in[:, 2 * D + c * 128:2 * D + (c + 1) * 128], NC, 128)
            w0, w1, w2 = [convw[:, c * 3 + k:c * 3 + k + 1] for k in range(3)]
            S.op("pool", lambda e, c=c: e.tensor_copy(ubufA[:, 0:1], ucarry[:, c, 0:1]), reads=["ucarry"], writes=[*bk("ubufA")])
            S.op("pool", lambda e, c=c: e.tensor_copy(ptmp[:, 0:2], ucarry[:, c, 1:3]), reads=["ucarry"], writes=[*bk("ptmp")])
            for ti, (off, nn) in enumerate(tiles):
                t = state["tog"]
                state["tog"] = (t + 1) % NSET
                ak = blk_keys("hT", off, nn)
                pcb = proj(wfb, wkb, hT, ak, NC, off, nn)
                pcc = proj(wfc, wkc, hT, ak, NC, off, nn)
                pcx = proj(wfx, wkx, hT, ak, NC, off, nn)
                S.op("act", lambda e, pcx=pcx, nn=nn: e.activation(out=cxs[:, 0:nn], in_=ps[pcx][:, 0:nn], func=ACT.Copy),
                     reads=[("ps", pcx)], writes=["cxs"])
                ut = t1b[t]
                S.op("dve", lambda e, pcc=pcc, nn=nn, ut=ut: e.tensor_tensor(ut[:, 0:nn], ps[pcc][:, 0:nn], cxs[:, 0:nn], ALU.mult),
                     reads=[("ps", pcc), "cxs"], writes=[("t1", t)])
                conv3(lambda ut=ut, nn=nn: ut[:, 0:nn], [("t1", t)], yA, ubufA, ptmp, "yA", "ubufA", "ptmp", w0, w1, w2, off, nn, ti)
                S.op("act", lambda e, pcb=pcb, off=off, nn=nn: e.activation(out=yB[:, off:off + nn], in_=ps[pcb][:, 0:nn], func=ACT.Copy),
                     reads=[("ps", pcb)], writes=[("yB", ti)])
                yield ("tile", c, ti)
            S.op("dve", lambda e: e.tensor_tensor(yA[:, 0:n], yA[:, 0:n], ubufA[:, 0:n], ALU.add), reads=[*bk("yA"), *bk("ubufA")], writes=[*bk("yA")])
            S.op("dve", lambda e: e.tensor_tensor(yA[:, 0:n], yA[:, 0:n], ptmp[:, 0:n], ALU.add), reads=[*bk("yA"), *bk("ptmp")], writes=[*bk("yA")])
            S.op("pool", lambda e, c=c: e.tensor_tensor(convy[:, c, 0:n], yA[:, 0:n], yB[:, 0:n], ALU.mult),
                 reads=[*bk("yA"), *bk("yB")], writes=[("cy", c)])
            S.op("pool", lambda e, c=c: e.tensor_copy(ucarry[:, c, 0:1], ubufA[:, n:n + 1]), reads=[*bk("ubufA")], writes=["ucarry"])
            S.op("pool", lambda e, c=c: e.tensor_copy(ucarry[:, c, 1:3], ptmp[:, n:n + 2]), reads=[*bk("ptmp")], writes=["ucarry"])
            yield c

    SC = 1.0 / math.sqrt(128.0)

    def attn_phase(l0, n, tiles, hs, bg=None):
        qi = state["qi"]
        state["qi"] = 1 - qi
        qT = qTs[qi]
        for g in range(3):
            gh = g * HPG + hs
            wf, wk = wget(w_in[:, 3 * D + gh * 128:3 * D + (gh + 1) * 128], NC, 128)
            for (off, nn) in tiles:
                pi = proj(wf, wk, hT, blk_keys("hT", off, nn), NC, off, nn)
                normrope(pi, nn, off, gq, qT[:, g, off:off + nn], [("qT", qi, g)], 1)
        if bg is not None:
            next(bg, None)
        flush()
        numacc, denacc = pt2[:, 0:n], ubufB[:, 0:n]
        S.op("pool", lambda e: e.memset(numacc, 0.0), writes=[*bk("pt2")])
        S.op("pool", lambda e: e.memset(denacc, 0.0), writes=[*bk("ubufB")])
        for g in range(3):
            dil = DILS[g]
            gh = g * HPG + hs
            if bg is not None and g > 0:
                next(bg, None)
            flush()
            kslot = state["ki"] % KDB
            state["ki"] += 1
            kTt = kTts[kslot]
            kkey = ("kTt", kslot)
            S.op("sp", lambda e, gh=gh, kTt=kTt: e.dma_start(out=kTt, in_=kcache[gh]), reads=[("kc", gh)], writes=[kkey], dma=True)
            S.op("sp", lambda e, g=g: e.dma_start(out=vtt, in_=vcache[g, :, :, hs * 128:(hs + 1) * 128].rearrange("b k d -> k b d")),
                 reads=[("vc", g * HPG + hs)], writes=["vtt"], dma=True)
            i0, i1 = l0 // dil, (l0 + n) // dil
            NBR = 32 // dil
            nbc = -16 // dil
            RL = 4096 // dil

            def kidx(r, i):
                return r * RL + i + 2048 // dil

            def bidx(r, nb):
                return r * NBR + (nb - nbc)

            chunks = []
            for nb in range(i0 // 128, (i1 - 1) // 128 + 1):
                qa = max(i0, 128 * nb) - 128 * nb
                qb = min(i1, 128 * nb + 128) - 128 * nb
                chunks.append((nb, qa, qb))
            batches = []
            if dil == 1:
                umax = 4
                for u0 in range(0, len(chunks), umax):
                    cs = chunks[u0:u0 + umax]
                    batches.append(([(0, c[0]) for c in cs], cs[0][1], cs[0][2]))
            else:
                for (nb, qa, qb) in chunks:
                    m = qb - qa
                    umax = max(1, 512 // m)
                    for r0 in range(0, dil, umax):
                        batches.append(([(r, nb) for r in range(r0, min(dil, r0 + umax))], qa, qb))
            for (units, qa, qb) in batches:
                m = qb - qa
                U = len(units)
                nb0 = units[0][1]
                has_prev = (nb0 - 1) >= NBMIN[g]
                psc = pnext()
                psp = pnext() if has_prev else None
                pti = state["pti"]
                state["pti"] = 1 - pti
                pTp_ = pTps[pti]
                pTc_ = pTcs[pti]
                kp_ = ("pTp", 0) if pti == 0 else ("vst", 0)
                kc2_ = ("pTc", 0) if pti == 0 else ("vst", 1)
                for u, (r, nb) in enumerate(units):
                    c0 = dil * (128 * nb + qa) + r - l0
                    qcols = qT[:, g, c0:c0 + dil * (m - 1) + 1:dil]
                    if has_prev:
                        kp = kTt[:, kidx(r, 128 * (nb - 1)):kidx(r, 128 * (nb - 1)) + 128]
                        S.op("pe", lambda e, psp=psp, u=u, kp=kp, qcols=qcols, m=m: e.matmul(ps[psp][:, u * m:(u + 1) * m], kp, qcols, start=True, stop=True),
                             reads=[kkey, ("qT", qi, g)], writes=[("ps", psp)])
                    kc_ = kTt[:, kidx(r, 128 * nb):kidx(r, 128 * nb) + qb]
                    S.op("pe", lambda e, psc=psc, u=u, kc_=kc_, qcols=qcols, m=m, qb=qb: e.matmul(ps[psc][0:qb, u * m:(u + 1) * m], kc_, qcols, start=True, stop=True),
                         reads=[kkey, ("qT", qi, g)], writes=[("ps", psc)])
                if has_prev:
                    S.op("act", lambda e, psp=psp, U=U, m=m, pTp_=pTp_: e.activation(out=pTp_[:, 0:U * m], in_=ps[psp][:, 0:U * m], func=ACT.Exp, scale=SC),
                         reads=[("ps", psp)], writes=[kp_])
                    mk = mprev[:, qa:qb].unsqueeze(1).broadcast_to([128, U, m])
                    S.op("dve", lambda e, U=U, m=m, mk=mk, pTp_=pTp_: e.tensor_tensor(pTp_[:, 0:U * m].rearrange("p (u j) -> p u j", u=U),
                                                                          pTp_[:, 0:U * m].rearrange("p (u j) -> p u j", u=U), mk, ALU.mult),
                         reads=[kp_, "cp"], writes=[kp_])
                S.op("act", lambda e, psc=psc, U=U, m=m, qb=qb, pTc_=pTc_: e.activation(out=pTc_[0:qb, 0:U * m], in_=ps[psc][0:qb, 0:U * m], func=ACT.Exp, scale=SC),
                     reads=[("ps", psc)], writes=[kc2_])
                mk2 = mcur[0:qb, qa:qb].unsqueeze(1).broadcast_to([qb, U, m])
                S.op("dve", lambda e, U=U, m=m, qb=qb, mk2=mk2, pTc_=pTc_: e.tensor_tensor(pTc_[0:qb, 0:U * m].rearrange("p (u j) -> p u j", u=U),
                                                                             pTc_[0:qb, 0:U * m].rearrange("p (u j) -> p u j", u=U), mk2, ALU.mult),
                     reads=[kc2_, "cp"], writes=[kc2_])
                def pv_part(units=units, qa=qa, qb=qb, m=m, U=U, nb0=nb0, has_prev=has_prev, pTp_=pTp_, pTc_=pTc_, kp_=kp_, kc2_=kc2_,
                            g=g, dil=dil, bidx=bidx):
                    pn = pnext()
                    pd = pnext()
                    for u, (r, nb) in enumerate(units):
                        tn, td = [], []
                        if has_prev:
                            tn.append((vtt[:, bidx(r, nb - 1), :], pTp_[:, u * m:(u + 1) * m], ["vtt", kp_]))
                            td.append(((hmat if nb - 1 < 0 else ones), pTp_[:, u * m:(u + 1) * m], ["cp", kp_]))
                        tn.append((vtt[0:qb, bidx(r, nb), :], pTc_[0:qb, u * m:(u + 1) * m], ["vtt", kc2_]))
                        td.append(((hmat if nb < 0 else ones)[0:qb, :], pTc_[0:qb, u * m:(u + 1) * m], ["cp", kc2_]))
                        mmgroup(ps[pn][:, u * m:(u + 1) * m], ("ps", pn), tn)
                        mmgroup(ps[pd][:, u * m:(u + 1) * m], ("ps", pd), td)
                    if dil == 1:
                        base = 128 * nb0 + qa - l0
                        nv = numacc[:, base:base + U * m]
                        dv = denacc[:, base:base + U * m]
                        pnv, pdv = ps[pn][:, 0:U * m], ps[pd][:, 0:U * m]
                    else:
                        base = dil * (128 * nb0 + qa) - l0
                        r0 = units[0][0]
                        nv = numacc[:, base:base + dil * m].rearrange("p (j r) -> p r j", r=dil)[:, r0:r0 + U, :]
                        dv = denacc[:, base:base + dil * m].rearrange("p (j r) -> p r j", r=dil)[:, r0:r0 + U, :]
                        pnv = ps[pn][:, 0:U * m].rearrange("p (u j) -> p u j", u=U)
                        pdv = ps[pd][:, 0:U * m].rearrange("p (u j) -> p u j", u=U)
                    S.op("dve", lambda e, nv=nv, pnv=pnv: e.tensor_tensor(nv, pnv, nv, ALU.add), reads=[("ps", pn), *bk("pt2")], writes=[*bk("pt2")])
                    S.op("dve", lambda e, dv=dv, pdv=pdv: e.tensor_tensor(dv, pdv, dv, ALU.add), reads=[("ps", pd), *bk("ubufB")], writes=[*bk("ubufB")])
                if bg is not None:
                    next(bg, None)
                flush()
                if "pvdefer" in DBG:
                    defer(pv_part)
                else:
                    pv_part()
        flush()
        rden = denacc
        S.op("dve", lambda e: e.tensor_scalar(denacc, denacc, 1e-30, None, ALU.add), reads=[*bk("ubufB")], writes=[*bk("ubufB")])
        S.op("dve", lambda e: e.reciprocal(rden, denacc), reads=[*bk("ubufB")], writes=[*bk("ubufB")])
        S.op("pool", lambda e: e.tensor_tensor(attnT[:, hs, 0:n], numacc, rden, ALU.mult), reads=[*bk("pt2"), *bk("ubufB")], writes=[("aT", hs)])

    def merge_phase(l0, n, tiles):
        cyk = [("cy", c) for c in range(NC)]
        atk = [("aT", h) for h in range(HPG)]
        for j in range(NC):
            wco, kco = wget(w_co[:, j * 128:(j + 1) * 128], NC, 128)
            wgc, kgc = wget(w_in[:, 3 * D + 3 * AW + j * 128:3 * D + 3 * AW + (j + 1) * 128], NC, 128)
            wao, kao = wget(w_ao[:, j * 128:(j + 1) * 128], HPG, 128)
            wga, kga = wget(w_in[:, 4 * D + 3 * AW + j * 128:4 * D + 3 * AW + (j + 1) * 128], NC, 128)
            for (off, nn) in tiles:
                t = state["tog"]
                state["tog"] = (t + 1) % NSET
                ak = blk_keys("hT", off, nn)
                pbc = proj(wco, kco, convy, cyk, NC, off, nn)
                pgc = proj(wgc, kgc, hT, ak, NC, off, nn)
                pba = proj(wao, kao, attnT, atk, HPG, off, nn)
                pga = proj(wga, kga, hT, ak, NC, off, nn)
                S.op("act", lambda e, pgc=pgc, t=t, nn=nn: e.activation(out=t1b[t][:, 0:nn], in_=ps[pgc][:, 0:nn], func=ACT.Sigmoid),
                     reads=[("ps", pgc)], writes=[("t1", t)])
                S.op("act", lambda e, pga=pga, t=t, nn=nn: e.activation(out=t2b[t][:, 0:nn], in_=ps[pga][:, 0:nn], func=ACT.Sigmoid),
                     reads=[("ps", pga)], writes=[("t2", t)])
                S.op("dve", lambda e, pbc=pbc, t=t, nn=nn: e.tensor_tensor(t1b[t][:, 0:nn], ps[pbc][:, 0:nn], t1b[t][:, 0:nn], ALU.mult),
                     reads=[("ps", pbc), ("t1", t)], writes=[("t1", t)])
                S.op("dve", lambda e, pba=pba, t=t, nn=nn: e.tensor_tensor(t2b[t][:, 0:nn], ps[pba][:, 0:nn], t2b[t][:, 0:nn], ALU.mult),
                     reads=[("ps", pba), ("t2", t)], writes=[("t2", t)])
                S.op("pool", lambda e, j=j, t=t, off=off, nn=nn: e.tensor_tensor(r3v[:, j, off:off + nn], t1b[t][:, 0:nn], t2b[t][:, 0:nn], ALU.add),
                     reads=[("t1", t), ("t2", t)], writes=[("mg", j)])

    def tokmajor_accum(wd_fn, kcn, actT_, akeys, nblk):
        for ct in range(D // 256):
            wf, wk = wget(wd_fn(ct), kcn, 256)
            for b in range(nblk):
                pi = pnext()
                mmgroup(ps[pi][:, 0:256], ("ps", pi),
                        [(actT_[:, k, b * 128:(b + 1) * 128], wf(k), list(wk) + list(akeys)) for k in range(kcn)])
                dst = x1v[:, b, ct * 256:(ct + 1) * 256]
                S.op("dve", lambda e, dst=dst, pi=pi: e.tensor_tensor(dst, ps[pi][:, 0:256], dst, ALU.add),
                     reads=[("ps", pi), ("x1", b)], writes=[("x1", b)])

    def ffn_phase(l0, n, tiles, nblk):
        h2k = [("h2", b) for b in range(nblk)]
        for fb in range(FB):
            for fi in range(FCB):
                f = fb * FCB + fi
                if fi % 2 == 0:
                    npair = min(2, FCB - fi)
                    wg, kg = wget(w_up[:, f * 128:(f + npair) * 128], NC, npair * 128)
                    wu, ku = wget(w_up[:, DFF + f * 128:DFF + (f + npair) * 128], NC, npair * 128)
                jj = fi % 2
                g0, g1, g2 = [fconvw[:, f * 3 + k:f * 3 + k + 1] for k in range(3)]
                v0, v1, v2 = [fconvw[:, (FC + f) * 3 + k:(FC + f) * 3 + k + 1] for k in range(3)]
                S.op("pool", lambda e, f=f: e.tensor_copy(ubufA[:, 0:1], fcarry[:, f, 0:1]), reads=["fcarry"], writes=[*bk("ubufA")])
                S.op("pool", lambda e, f=f: e.tensor_copy(ptmp[:, 0:2], fcarry[:, f, 1:3]), reads=["fcarry"], writes=[*bk("ptmp")])
                S.op("pool", lambda e, f=f: e.tensor_copy(ubufB[:, 0:1], fcarry[:, FC + f, 0:1]), reads=["fcarry"], writes=[*bk("ubufB")])
                S.op("pool", lambda e, f=f: e.tensor_copy(pt2[:, 0:2], fcarry[:, FC + f, 1:3]), reads=["fcarry"], writes=[*bk("pt2")])
                for ti, (off, nn) in enumerate(tiles):
                    ak = blk_keys("h2", off, nn)
                    pg = proj(wg, kg, r3v, ak, NC, off, nn, j=jj)
                    pu = proj(wu, ku, r3v, ak, NC, off, nn, j=jj)
                    conv3(lambda pg=pg, nn=nn: ps[pg][:, 0:nn], [("ps", pg)], yA, ubufA, ptmp, "yA", "ubufA", "ptmp", g0, g1, g2, off, nn, ti)
                    conv3(lambda pu=pu, nn=nn: ps[pu][:, 0:nn], [("ps", pu)], yB, ubufB, pt2, "yB", "ubufB", "pt2", v0, v1, v2, off, nn, ti)
                S.op("dve", lambda e: e.tensor_tensor(yA[:, 0:n], yA[:, 0:n], ubufA[:, 0:n], ALU.add), reads=[*bk("yA"), *bk("ubufA")], writes=[*bk("yA")])
                S.op("dve", lambda e: e.tensor_tensor(yA[:, 0:n], yA[:, 0:n], ptmp[:, 0:n], ALU.add), reads=[*bk("yA"), *bk("ptmp")], writes=[*bk("yA")])
                S.op("pool", lambda e: e.tensor_tensor(yB[:, 0:n], yB[:, 0:n], ubufB[:, 0:n], ALU.add), reads=[*bk("yB"), *bk("ubufB")], writes=[*bk("yB")])
                S.op("pool", lambda e: e.tensor_tensor(yB[:, 0:n], yB[:, 0:n], pt2[:, 0:n], ALU.add), reads=[*bk("yB"), *bk("pt2")], writes=[*bk("yB")])
                S.op("pool", lambda e, f=f: e.tensor_copy(fcarry[:, f, 0:1], ubufA[:, n:n + 1]), reads=[*bk("ubufA")], writes=["fcarry"])
                S.op("pool", lambda e, f=f: e.tensor_copy(fcarry[:, f, 1:3], ptmp[:, n:n + 2]), reads=[*bk("ptmp")], writes=["fcarry"])
                S.op("pool", lambda e, f=f: e.tensor_copy(fcarry[:, FC + f, 0:1], ubufB[:, n:n + 1]), reads=[*bk("ubufB")], writes=["fcarry"])
                S.op("pool", lambda e, f=f: e.tensor_copy(fcarry[:, FC + f, 1:3], pt2[:, n:n + 2]), reads=[*bk("pt2")], writes=["fcarry"])
                S.op("act", lambda e: e.activation(out=yA[:, 0:n], in_=yA[:, 0:n], func=ACT.Silu), reads=[*bk("yA")], writes=[*bk("yA")])
                S.op("dve", lambda e, fi=fi: e.tensor_tensor(actT[:, fi, 0:n], yA[:, 0:n], yB[:, 0:n], ALU.mult),
                     reads=[*bk("yA"), *bk("yB")], writes=[("ac", fi)])
            tokmajor_accum(lambda ct, fb=fb: w_dn[fb * FCB * 128:(fb + 1) * FCB * 128, ct * 256:(ct + 1) * 256],
                           FCB, actT, [("ac", k) for k in range(FCB)], nblk)

    fz = sb("fz", [128, 1], F32)

    def fence(reads, writes):
        S.op("pool", lambda e: e.memset(fz[:], 0.0), reads=list(reads), writes=list(writes) + ["fz"])

    def load_x_block(l0):
        def f(b, xi):
            r0 = l0 + 2048 + 128 * b
            S.op("sp", lambda e: e.dma_start(out=xbs[xi][:], in_=xe[r0:r0 + 128, :]), writes=[("xb", xi)], dma=True)
        return f

    def zero_fill():
        S.op("dve", lambda e: e.memset(r3[:, 0:4096], 0.0), writes=[("kTt", 0)])
        zsrc = r3[:, 0:4096].rearrange("p (b d) -> p b d", b=4)[:, :, 0:HPG * 128]
        for g in (2, 1, 0):
            for h in range(HPG):
                gh = g * HPG + h
                S.op("pool", lambda e, gh=gh: e.dma_start(out=kcache[gh], in_=r3[:, 0:4096]), reads=[("kTt", 0)], writes=[("kc", gh)], dma=True)
            for b0 in range(0, 32, 4):
                S.op("pool", lambda e, g=g, b0=b0: e.dma_start(out=vcache[g, b0:b0 + 4].rearrange("b k d -> k b d"), in_=zsrc),
                     reads=[("kTt", 0)], writes=[("vc", g * HPG + h) for h in range(HPG)], dma=True)

    class _Stop(Exception):
        pass

    stg = {"n": 0}

    def stage(name):
        stg["n"] += 1
        if getattr(cfg, "max_stage", None) is not None and stg["n"] > cfg.max_stage:
            print("STOP before stage", stg["n"], name)
            raise _Stop()

    big_keys = [("hT", b) for b in range(NB)] + [("cy", c) for c in range(NC)]
    x1_keys = [("x1", b) for b in range(NB)]
    r3_attn = [("kTt", 0), ("kTt", 1), "vtt"]
    mg_keys = [("mg", j) for j in range(NC)]
    h2_keys = [("h2", b) for b in range(NB)]
    r4_attn = [("aT", h) for h in range(HPG)] + [("qT", i, g) for g in range(3) for i in range(2)]
    ac_keys = [("ac", k) for k in range(FCB)]
    out_keys = []

    def emit():
        hU = convy
        sts = list(cfg.halo_sts)
        nh = len(sts)
        bufs = [(hU, "hU") if (nh - i) % 2 == 1 else (hT, "hT") for i in range(nh)] + [(hT, "hT")]
        allst = sts + [cfg.main_sts[0]]

        def mk_gen(i):
            l0_, n_ = allst[i]
            return build_hT(lambda b, xi: xbs[xi][:], lambda b, xi: [("xb", xi)], mixg, bufs[i][0], bufs[i][1], n_ // 128,
                            load_fn=load_x_block(l0_))

        stage("halo first hT")
        zero_fill()
        run(mk_gen(0))
        for i, (l0, n) in enumerate(sts):
            stage("halo tables %d" % l0)
            build_tables(l0, n)
            stage("halo kv")
            bg = mk_gen(i + 1)
            kv_phase(l0, n, bufs[i][0], bufs[i][1], bg=bg)
            run(bg)
        for (l0, n) in cfg.main_sts:
            nblk = n // 128
            tiles = [(tl - l0, nn) for (tl, nn) in split_tiles(l0, l0 + n)]
            stage("main hT %d" % l0)
            if (l0, n) == cfg.main_sts[0]:
                fence(x1_keys + [("hU", b) for b in range(NB)], [("cy", c) for c in range(NC)])
            else:
                fence(x1_keys, big_keys)
                run(build_hT(lambda b, xi: xbs[xi][:], lambda b, xi: [("xb", xi)], mixg, hT, "hT", nblk, load_fn=load_x_block(l0)))
            build_tables(l0, n)
            stage("main kv")
            kv_phase(l0, n, hT, "hT")
            stage("main conv+attn")
            fence(h2_keys + mg_keys, r3_attn)
            fence(ac_keys, r4_attn)
            cg = conv_phase(l0, n, tiles)
            for hs in range(HPG):
                attn_phase(l0, n, tiles, hs, bg=cg)
            run(cg)
            stage("main merge")
            fence(r3_attn, mg_keys)
            merge_phase(l0, n, tiles)
            stage("main mergeout")
            fence(big_keys, x1_keys)
            for b in range(nblk):
                r0 = l0 + 2048 + 128 * b
                S.op("sp", lambda e, b=b, r0=r0: e.dma_start(out=x1v[:, b, :], in_=xe[r0:r0 + 128, :]), writes=[("x1", b)], dma=True)
            tokmajor_accum(lambda ct: w_mo[:, ct * 256:(ct + 1) * 256], NC, r3v, mg_keys, nblk)
            stage("main h2")
            fence(mg_keys, h2_keys)
            run(build_hT(lambda b, xi: x1v[:, b, :], lambda b, xi: [("x1", b)], ffng, r3v, "h2", nblk))
            stage("main ffn")
            fence(r4_attn, ac_keys)
            ffn_phase(l0, n, tiles, nblk)
            stage("main out")
            for b in range(nblk):
                l = l0 + 128 * b
                if l < 0:
                    continue
                S.op("sp", lambda e, b=b, l=l: e.dma_start(out=out[l:l + 128, :], in_=x1v[:, b, :]), reads=[("x1", b)], writes=[("out", l)], dma=True)
                out_keys.append(("out", l))

    try:
        emit()
    except _Stop:
        pass
    S.op("sp", lambda e: None, reads=out_keys)
    S.build()
    return nc, S


def host_consts(cfg, is_first_half):
    f32 = np.float32
    ident = np.eye(128, dtype=f32)
    RTm = np.zeros((128, 128), f32)
    for m in range(16):
        RTm[m + 16, m] = -1.0
        RTm[m, m + 16] = 1.0
    kk = np.arange(128)[:, None]
    qq = np.arange(128)[None, :]
    mprev = (kk >= qq).astype(f32)
    mcur = (kk <= qq).astype(f32)
    ones = np.ones((128, 128), f32)
    hmat = np.zeros((128, 128), f32) if is_first_half else ones
    return np.concatenate([ident, RTm, mprev, mcur, ones, hmat], axis=1)


def host_smallp(cfg, mix_norm, ffn_norm, conv_mix_w, ffn_conv_w, q_norm, k_norm):
    NC, FC = cfg.NC, cfg.FC
    sp = np.zeros((128, cfg.NSP), np.float32)
    sp[:, 0:NC] = mix_norm.reshape(NC, 128).T
    sp[:, NC:2 * NC] = ffn_norm.reshape(NC, 128).T
    sp[:, 2 * NC:5 * NC] = conv_mix_w.reshape(3, NC, 128).transpose(2, 1, 0).reshape(128, NC * 3)
    sp[:, 5 * NC:5 * NC + 6 * FC] = ffn_conv_w.reshape(3, 2 * FC, 128).transpose(2, 1, 0).reshape(128, 2 * FC * 3)
    o0 = 5 * NC + 6 * FC
    sp[:, o0] = q_norm
    sp[:, o0 + 1] = k_norm
    half = 16
    invf = np.zeros(128, np.float32)
    fr = (np.float32(500000.0) ** (-np.arange(half, dtype=np.float32) * np.float32(2.0 / 32))).astype(np.float32)
    invf[0:16] = fr
    invf[16:32] = fr
    sp[:, o0 + 2] = invf
    return sp


def make_in_maps(cfg, x, positions, mix_norm, w_in, conv_mix_w, w_conv_out, q_norm, k_norm, w_attn_out,
                 w_merge_out, ffn_norm, w_up, ffn_conv_w, w_down):
    B = x.shape[0]
    D = cfg.D
    sp = host_smallp(cfg, mix_norm[0], ffn_norm[0], conv_mix_w[0], ffn_conv_w[0], q_norm[0], k_norm[0])
    shared = {
        "w_in": np.ascontiguousarray(w_in[0]), "w_co": np.ascontiguousarray(w_conv_out[0]),
        "w_ao": np.ascontiguousarray(w_attn_out[0]), "w_mo": np.ascontiguousarray(w_merge_out[0]),
        "w_up": np.ascontiguousarray(w_up[0]), "w_dn": np.ascontiguousarray(w_down[0]), "smallp": sp,
    }
    maps = []
    for core in range(2 * B):
        b, half = core // 2, core % 2
        T0 = 2048 * half
        xe = np.zeros((4096, D), np.float32)
        pe = np.zeros((1, 4096), np.int32)
        lo = T0 - 2048
        if lo >= 0:
            xe[:] = x[b, lo:lo + 4096]
            pe[0] = positions[b, lo:lo + 4096]
        else:
            xe[2048:] = x[b, 0:2048]
            pe[0, 2048:] = positions[b, 0:2048]
        m = dict(shared)
        m["xe"] = xe
        m["pos"] = pe
        m["cpack"] = host_consts(cfg, half == 0)
        maps.append(m)
    return maps


_CACHE = {}


def kernel(x, positions, mix_norm, w_in, conv_mix_w, w_conv_out, q_norm, k_norm, w_attn_out,
           w_merge_out, ffn_norm, w_up, ffn_conv_w, w_down):
    cfg = Cfg()
    args = [np.asarray(a) for a in (x, positions, mix_norm, w_in, conv_mix_w, w_conv_out, q_norm, k_norm,
                                    w_attn_out, w_merge_out, ffn_norm, w_up, ffn_conv_w, w_down)]
    maps = make_in_maps(cfg, *args)
    if "nc" not in _CACHE:
        _CACHE["nc"] = build_program(cfg)[0]
    nc = _CACHE["nc"]
    res = run_bass_kernel_spmd(nc, maps, core_ids=list(range(8)))
    B = args[0].shape[0]
    outp = np.zeros((B, 4096, cfg.D), np.float32)
    for core in range(8):
        b, half = core // 2, core % 2
        outp[b, 2048 * half:2048 * (half + 1)] = res.results[core]["out"]
    return outp
```
